# Optimizing a Trainium2 kernel written in Bass

```python
import jax, jax.numpy as jnp
from jax import lax
import numpy as np

D_MODEL = 1024
BATCH = 8
SEQ = 2048
DEPTH = 1

CHUNK = 64
D_SHORT = D_MODEL
SHORT_CONV = 3
SSM_EXPAND = 2
D_INNER = SSM_EXPAND * D_MODEL
SSM_HEAD_DIM = 64
SSM_HEADS = D_INNER // SSM_HEAD_DIM
SSM_GROUPS = 8
D_STATE = 128
SSM_CONV = 4
D_XBC = D_INNER + 2 * SSM_GROUPS * D_STATE
D_FF = 2816
NORM_EPS = 1e-5

kernel_name = "hybrid_shortconv_ssd_macaron_block"


def rms_norm(x, w):
    xf = x.astype(jnp.float32)
    y = xf * lax.rsqrt(jnp.mean(xf * xf, axis=-1, keepdims=True) + NORM_EPS)
    return (y * w.astype(jnp.float32)).astype(x.dtype)


def gated_group_rms_norm(y, z, w, groups):
    yf = y.astype(jnp.float32) * jax.nn.silu(z.astype(jnp.float32))
    shp = yf.shape
    yg = yf.reshape(shp[:-1] + (groups, shp[-1] // groups))
    yg = yg * lax.rsqrt(jnp.mean(yg * yg, axis=-1, keepdims=True) + NORM_EPS)
    return (yg.reshape(shp) * w.astype(jnp.float32)).astype(y.dtype)


def swiglu(h, w_in, w_out):
    g, u = jnp.split(h @ w_in, 2, axis=-1)
    return (jax.nn.silu(g) * u) @ w_out


def causal_dwconv(x, w):
    k = w.shape[0]
    return lax.conv_general_dilated(
        x, w[:, None, :].astype(x.dtype), window_strides=(1,),
        padding=[(k - 1, 0)], dimension_numbers=("NWC", "WIO", "NWC"),
        feature_group_count=x.shape[-1])


def segsum(a):
    t = a.shape[-1]
    ae = jnp.broadcast_to(a[..., None], a.shape + (t,))
    ae = jnp.where(jnp.tril(jnp.ones((t, t), bool), -1), ae, 0.0)
    ss = jnp.cumsum(ae, axis=-2)
    return jnp.where(jnp.tril(jnp.ones((t, t), bool), 0), ss, -jnp.inf)


def ssd_scan(x, dt, a_coef, bm, cm):
    b, s, h, p = x.shape
    g, n = bm.shape[-2:]
    e = h // g
    c = s // CHUNK
    xdt = (x * dt[..., None]).reshape(b, c, CHUNK, g, e, p)
    bc = bm.reshape(b, c, CHUNK, g, n)
    cc = cm.reshape(b, c, CHUNK, g, n)
    a = (dt * a_coef).reshape(b, c, CHUNK, g, e).transpose(0, 3, 4, 1, 2)
    a_cs = jnp.cumsum(a, axis=-1)
    decay = jnp.exp(segsum(a))
    cb = jnp.einsum("bclgn,bcsgn->bgcls", cc, bc)
    y_diag = jnp.einsum("bgecls,bcsgep->bclgep", cb[:, :, None] * decay, xdt)
    decay_states = jnp.exp(a_cs[..., -1:] - a_cs)
    states = jnp.einsum("bclgn,bgecl,bclgep->bcgepn", bc, decay_states, xdt)
    states = jnp.concatenate([jnp.zeros_like(states[:, :1]), states], axis=1)
    chunk_decay = jnp.exp(segsum(jnp.pad(a_cs[..., -1], ((0, 0), (0, 0), (0, 0), (1, 0)))))
    states = jnp.einsum("bgezc,bcgepn->bzgepn", chunk_decay, states)[:, :-1]
    y_off = jnp.einsum("bclgn,bcgepn,bgecl->bclgep", cc, states, jnp.exp(a_cs))
    return (y_diag + y_off).reshape(b, s, h, p)


def setup_inputs(seed: int = 0) -> dict:
    key = jax.random.key(seed)
    ks = jax.random.split(key, 24)
    L, D = DEPTH, D_MODEL
    n_in = 3 * D_SHORT + D_INNER + D_XBC + SSM_HEADS + 2 * D_MODEL

    def nrm(k, shape, fan_in):
        return jax.random.normal(k, shape, jnp.float32) * fan_in ** -0.5

    def gain(k, shape):
        return 1.0 + 0.02 * jax.random.normal(k, shape, jnp.float32)

    dt0 = jnp.exp(jax.random.uniform(ks[10], (L, SSM_HEADS), jnp.float32,
                                     np.log(1e-3), np.log(1e-1)))
    dt_bias = dt0 + jnp.log(-jnp.expm1(-dt0))
    a_log = jnp.log(jax.random.uniform(ks[11], (L, SSM_HEADS), jnp.float32, 1.0, 16.0))
    return {
        "x": jax.random.normal(ks[0], (BATCH, SEQ, D), jnp.float32),
        "ffn1_norm": gain(ks[1], (L, D)),
        "ffn1_w_in": nrm(ks[2], (L, D, 2 * D_FF), D),
        "ffn1_w_out": nrm(ks[3], (L, D_FF, D), D_FF),
        "mix_norm": gain(ks[4], (L, D)),
        "w_in": nrm(ks[5], (L, D, n_in), D),
        "short_conv_w": nrm(ks[6], (L, SHORT_CONV, D_SHORT), SHORT_CONV),
        "short_w_out": nrm(ks[7], (L, D_SHORT, D), D_SHORT),
        "ssm_conv_w": nrm(ks[8], (L, SSM_CONV, D_XBC), SSM_CONV),
        "ssm_conv_b": 0.02 * jax.random.normal(ks[9], (L, D_XBC), jnp.float32),
        "ssm_dt_bias": dt_bias,
        "ssm_A_log": a_log,
        "ssm_D": gain(ks[12], (L, SSM_HEADS)),
        "ssm_norm": gain(ks[13], (L, D_INNER)),
        "ssm_w_out": nrm(ks[14], (L, D_INNER, D), D_INNER),
        "w_out": nrm(ks[15], (L, D, D), D),
        "ffn2_norm": gain(ks[16], (L, D)),
        "ffn2_w_in": nrm(ks[17], (L, D, 2 * D_FF), D),
        "ffn2_w_out": nrm(ks[18], (L, D_FF, D), D_FF),
        "final_norm": gain(ks[19], (D,)),
    }


def reference(x, ffn1_norm, ffn1_w_in, ffn1_w_out, mix_norm, w_in, short_conv_w,
              short_w_out, ssm_conv_w, ssm_conv_b, ssm_dt_bias, ssm_A_log, ssm_D,
              ssm_norm, ssm_w_out, w_out, ffn2_norm, ffn2_w_in, ffn2_w_out,
              final_norm):
    b, s, _ = x.shape
    sizes = [D_SHORT, D_SHORT, D_SHORT, D_INNER, D_XBC, SSM_HEADS, D_MODEL, D_MODEL]
    cuts = [int(v) for v in np.cumsum(sizes)[:-1]]
    for l in range(DEPTH):
        x = x + 0.5 * swiglu(rms_norm(x, ffn1_norm[l]), ffn1_w_in[l], ffn1_w_out[l])

        h = rms_norm(x, mix_norm[l])
        b_gate, c_gate, xa, z, xbc, dt_raw, ga, gb = jnp.split(h @ w_in[l], cuts, axis=-1)

        va = causal_dwconv(c_gate * xa, short_conv_w[l])
        y_a = (b_gate * va) @ short_w_out[l]

        xbc = jax.nn.silu(causal_dwconv(xbc, ssm_conv_w[l]) + ssm_conv_b[l])
        xs, bm, cm = jnp.split(xbc, [D_INNER, D_INNER + SSM_GROUPS * D_STATE], axis=-1)
        xs = xs.reshape(b, s, SSM_HEADS, SSM_HEAD_DIM).astype(jnp.float32)
        bm = bm.reshape(b, s, SSM_GROUPS, D_STATE).astype(jnp.float32)
        cm = cm.reshape(b, s, SSM_GROUPS, D_STATE).astype(jnp.float32)
        dt = jax.nn.softplus(dt_raw.astype(jnp.float32) + ssm_dt_bias[l].astype(jnp.float32))
        a_coef = -jnp.exp(ssm_A_log[l].astype(jnp.float32))
        y_ssm = ssd_scan(xs, dt, a_coef, bm, cm) + xs * ssm_D[l].astype(jnp.float32)[:, None]
        y_ssm = y_ssm.reshape(b, s, D_INNER).astype(x.dtype)
        y_b = gated_group_rms_norm(y_ssm, z, ssm_norm[l], SSM_GROUPS) @ ssm_w_out[l]

        merged = jax.nn.sigmoid(ga) * y_a + jax.nn.sigmoid(gb) * y_b
        x = x + merged @ w_out[l]

        x = x + 0.5 * swiglu(rms_norm(x, ffn2_norm[l]), ffn2_w_in[l], ffn2_w_out[l])
    return rms_norm(x, final_norm)
```

```python
import numpy as np
import concourse.bass as bass
import concourse.mybir as mybir

F32 = mybir.dt.float32
BF16 = mybir.dt.bfloat16
ACTF = mybir.ActivationFunctionType
ALU = mybir.AluOpType

_DSZ = {F32: 4, BF16: 2, mybir.dt.float32r: 4, mybir.dt.int32: 4, mybir.dt.uint32: 4,
        mybir.dt.uint8: 1, mybir.dt.int8: 1, mybir.dt.uint16: 2, mybir.dt.int16: 2,
        mybir.dt.float16: 2}


def dsize(dt):
    return _DSZ[dt]


def footprint(ap):
    t = ap.tensor
    esz = dsize(ap.dtype)
    shape = list(t.shape)
    pstride = 1
    for s in shape[1:]:
        pstride *= int(s)
    tesz = dsize(t.dtype)
    pstride = pstride * tesz // esz
    aps = [(int(s), int(c)) for s, c in ap.ap]
    off = int(ap.offset)
    p0 = off // pstride
    foff = off % pstride
    pstep, pcnt = aps[0]
    if pcnt > 1:
        assert pstep == pstride, (pstep, pstride, ap)
    p1 = p0 + pcnt
    dims = [(s, c) for s, c in aps[1:] if c > 1]
    ivs = [(foff, foff + 1)]
    dims.sort(key=lambda sc: abs(sc[0]))
    for s, c in dims:
        s = abs(s)
        if s == 0:
            continue
        new = []
        lo0, hi0 = ivs[0][0], ivs[-1][1]
        width = hi0 - lo0
        if len(ivs) == 1 and s <= width:
            ivs = [(lo0, lo0 + s * (c - 1) + width)]
            continue
        if len(ivs) * c > 64:
            ivs = [(lo0, hi0 + s * (c - 1))]
            continue
        for i in range(c):
            for lo, hi in ivs:
                new.append((lo + i * s, hi + i * s))
        new.sort()
        ivs = new
    ivs = [(lo * esz, hi * esz) for lo, hi in ivs]
    return (t.name, p0, p1, ivs)


class Op:
    __slots__ = ("eng", "fn", "deps", "marked", "tok", "is_dma", "seq", "inc", "name", "soft", "cost", "nbytes")


class Prog:
    ENGS = ("pe", "act", "dve", "pool", "sp")

    def __init__(self, nc, n_dma_sems=20):
        self.nc = nc
        self.ops = []
        self.live = {}
        self.n_dma_sems = n_dma_sems
        self.dma_rr = {e: 0 for e in self.ENGS}
        self.dma_last = {}
        self.final_deps = []

    def _deps_for(self, o, ap, is_write, deps):
        name, p0, p1, ivs = footprint(ap)
        recs = self.live.setdefault(name, [])
        if name.startswith("ps"):
            for r in recs:
                if r[3] is not o:
                    self._add_dep(o, r[3], raw=(r[4] and not is_write), deps=deps)
            keep = [r for r in recs if r[3] is o]
            if not keep or is_write:
                keep = [(0, 128, [(0, 2048)], o, is_write or any(r[4] for r in keep))]
            self.live[name] = keep
            return
        lo_all, hi_all = ivs[0][0], ivs[-1][1]
        keep = []
        for r in recs:
            rp0, rp1, rivs, rop, rw = r
            if rop is o:
                keep.append(r)
                continue
            ov = False
            if rp0 < p1 and p0 < rp1 and rivs[0][0] < hi_all and lo_all < rivs[-1][1]:
                for lo, hi in ivs:
                    for rlo, rhi in rivs:
                        if rlo < hi and lo < rhi:
                            ov = True
                            break
                    if ov:
                        break
            if ov and (is_write or rw):
                self._add_dep(o, rop, raw=(rw and not is_write), deps=deps)
            if is_write and ov and rp0 >= p0 and rp1 <= p1:
                covered = True
                for rlo, rhi in rivs:
                    c1 = False
                    for lo, hi in ivs:
                        if lo <= rlo and rhi <= hi:
                            c1 = True
                            break
                    if not c1:
                        covered = False
                        break
                if covered:
                    continue
            keep.append(r)
        if not is_write:
            k2 = []
            for r in keep:
                if (not r[4]) and r[3].eng == o.eng and (not r[3].is_dma) and (not o.is_dma) \
                        and r[0] == p0 and r[1] == p1 and r[2] == ivs:
                    if r[3] is not o:
                        o.soft.add(r[3])
                    continue
                k2.append(r)
            keep = k2
        keep.append((p0, p1, ivs, o, is_write))
        self.live[name] = keep

    def _add_dep(self, o, d, raw, deps):
        if d is o:
            return
        if d.eng == o.eng and not d.is_dma and not o.is_dma:
            if o.eng == "pe":
                o.soft.add(d)
                return
        deps.add(d)

    def op(self, eng, fn, reads=(), writes=(), dma=False, name=None, cost=None):
        o = Op()
        o.eng = eng
        o.fn = fn
        o.is_dma = dma
        o.marked = False
        o.tok = None
        o.inc = 16 if dma else 1
        o.seq = len(self.ops)
        o.name = name
        o.soft = set()
        o.nbytes = 0
        o.cost = self._cost(eng, reads, writes, dma, o) if cost is None else cost
        deps = set()
        for ap in reads:
            self._deps_for(o, ap, False, deps)
        for ap in writes:
            self._deps_for(o, ap, True, deps)
        if dma:
            slot = self.dma_rr[eng] % self.n_dma_sems
            self.dma_rr[eng] += 1
            prev = self.dma_last.get((eng, slot))
            if prev is not None:
                deps.add(prev)
            self.dma_last[(eng, slot)] = o
            o.tok = ("dma", eng, slot)
            o.marked = True
        o.deps = sorted(deps, key=lambda d: d.seq)
        for d in o.deps:
            d.marked = True
        self.ops.append(o)
        return o

    @staticmethod
    def _fsize(ap):
        n = 1
        for d in ap.shape[1:]:
            n *= int(d)
        return n

    def _cost(self, eng, reads, writes, dma, o):
        aps = list(writes) if writes else list(reads)
        n = self._fsize(aps[0]) if aps else 1
        if dma:
            o.nbytes = n * int(aps[0].shape[0]) * 4
            return 1000.0 if eng == "pool" else 80.0
        if eng == "pe":
            mult = 4.0 if (reads and reads[0].dtype == F32) else 1.0
            return 10.0 + max(64, n) * mult / 2.15
        if eng == "act":
            return 220.0 + n / 1.4
        if eng == "dve":
            return (230.0 if (reads and reads[0].tensor.name.startswith("ps")) else 130.0) + n / 0.96
        if eng == "pool":
            return 520.0 + n * 1.3
        return 50.0

    def schedule(self):
        import heapq
        ops = self.ops
        n = len(ops)
        idx = {id(o): i for i, o in enumerate(ops)}
        succ = [[] for _ in range(n)]
        ndep = [0] * n
        for i, o in enumerate(ops):
            ds = set(o.deps) | o.soft
            ndep[i] = len(ds)
            for d in ds:
                succ[idx[id(d)]].append(i)
        plen = [0.0] * n
        for i in range(n - 1, -1, -1):
            m_ = 0.0
            for j in succ[i]:
                if plen[j] > m_:
                    m_ = plen[j]
            plen[i] = m_ + ops[i].cost + (2000.0 if ops[i].is_dma else 60.0)
        if PRIO_WINDOW:
            key = [(-(plen[i]) + PRIO_WINDOW * 0.0, i) for i in range(n)]
        rank = sorted(range(n), key=lambda i: (-plen[i], i)) if PRIO_CP else list(range(n))
        prio = [0] * n
        for r_, i in enumerate(rank):
            prio[i] = r_
        pend = {e: [] for e in self.ENGS}
        rdy = {e: [] for e in self.ENGS}
        free_at = {e: 0.0 for e in self.ENGS}
        start = [0.0] * n
        events = []
        for i, o in enumerate(ops):
            if ndep[i] == 0:
                heapq.heappush(rdy[o.eng], (prio[i], i))
        heapq.heappush(events, (0.0, -1))
        bus_free = 0.0
        done = 0
        BW = 160.0
        while events:
            T, ci = heapq.heappop(events)
            batch = [ci]
            while events and events[0][0] <= T:
                batch.append(heapq.heappop(events)[1])
            for c in batch:
                if c < 0:
                    continue
                for j in succ[c]:
                    ndep[j] -= 1
                    if ndep[j] == 0:
                        heapq.heappush(rdy[ops[j].eng], (prio[j], j))
            for e in self.ENGS:
                while free_at[e] <= T and rdy[e]:
                    i = heapq.heappop(rdy[e])[1]
                    o = ops[i]
                    start[i] = T
                    end = T + o.cost
                    free_at[e] = end
                    comp = end + 60.0
                    if o.is_dma:
                        b0 = max(end, bus_free)
                        bus_free = b0 + o.nbytes / BW
                        comp = bus_free + 1800.0
                    heapq.heappush(events, (comp, i))
                    if rdy[e]:
                        heapq.heappush(events, (end, -1))
                    done += 1
                    break
        assert done == n, (done, n)
        order = sorted(range(n), key=lambda i: (start[i], i))
        self.ops = [ops[i] for i in order]
        self.sim_time = max(start) if n else 0.0

    def dma(self, eng, out, in_, reads=(), writes=(), **kw):
        def fn(e, out=out, in_=in_, kw=kw):
            return e.dma_start(out=out, in_=in_, **kw)
        return self.op(eng, fn, reads=reads, writes=writes, dma=True)

    def emit(self, es):
        nc = self.nc
        sems = {e: es.enter_context(nc.semaphore("s_" + e)) for e in ("pe", "act", "dve", "pool")}
        dsem = {}
        for e in self.ENGS:
            if self.dma_rr[e] > 0:
                for s in range(min(self.n_dma_sems, self.dma_rr[e])):
                    dsem[(e, s)] = es.enter_context(nc.semaphore("d_%s_%d" % (e, s)))
        dcnt = {k: 0 for k in dsem}
        for o in self.ops:
            if o.is_dma:
                k = (o.tok[1], o.tok[2])
                dcnt[k] += 16
                o.tok = (dsem[k], dcnt[k])
        per = {e: [o for o in self.ops if o.eng == e] for e in self.ENGS}
        pos = {}
        for e_, lst in per.items():
            for i, o in enumerate(lst):
                pos[id(o)] = i
        need = {}
        marked = set()
        for e_ in self.ENGS:
            waited_pos = {}
            waited_dma = {}
            for o in per[e_]:
                best = {}
                w = []
                for d in o.deps:
                    if d.is_dma:
                        sem, val = d.tok
                        if waited_dma.get(id(sem), 0) < val:
                            waited_dma[id(sem)] = val
                            w.append(d)
                    else:
                        b = best.get(d.eng)
                        if b is None or pos[id(d)] > pos[id(b)]:
                            best[d.eng] = d
                for de, d in best.items():
                    if waited_pos.get(de, -1) < pos[id(d)]:
                        waited_pos[de] = pos[id(d)]
                        w.append(d)
                        marked.add(id(d))
                need[id(o)] = w
        cnt = {e: 0 for e in sems}
        for e_ in sems:
            for o in per[e_]:
                if (not o.is_dma) and id(o) in marked:
                    cnt[e_] += 1
                    o.tok = (sems[e_], cnt[e_])
        self.max_counts = dict(cnt)
        block = es.enter_context(nc.Block())

        def run(e, ename):
            for o in per[ename]:
                for d in need[id(o)]:
                    sem, val = d.tok
                    e.wait_ge(sem, val)
                if o.fn is None:
                    continue
                ins = o.fn(e)
                if o.is_dma or id(o) in marked:
                    ins.then_inc(o.tok[0], o.inc)

        if per["sp"]:
            @block.sync
            def _(e):
                run(e, "sp")
        if per["act"]:
            @block.scalar
            def _(e):
                run(e, "act")
        if per["dve"]:
            @block.vector
            def _(e):
                run(e, "dve")
        if per["pool"]:
            @block.gpsimd
            def _(e):
                run(e, "pool")
        if per["pe"]:
            @block.tensor
            def _(e):
                run(e, "pe")

    def fence(self, eng, ops):
        o = Op()
        o.eng = eng
        o.fn = None
        o.is_dma = False
        o.marked = False
        o.tok = None
        o.inc = 1
        o.seq = len(self.ops)
        o.name = "fence"
        o.soft = set()
        o.cost = 0.0
        o.nbytes = 0
        o.deps = sorted(set(ops), key=lambda d: d.seq)
        for d in o.deps:
            d.marked = True
        self.ops.append(o)
        return o


from contextlib import ExitStack
from concourse.bass_utils import run_bass_kernel_spmd

T = 2048
D = 1024
KD = 8
FF = 2816
NF = 22
NIN = 11296
EPS = 1e-5
SCHED = True
PRIO_WINDOW = 0
PRIO_CP = True
C_B, C_C, C_XA, C_Z, C_XS, C_BM, C_CM, C_DT, C_GA, C_GB = 0, 1024, 2048, 3072, 5120, 7168, 8192, 9216, 9248, 10272
V_N1, V_NM, V_N2, V_NF, V_SCW, V_CW, V_CB, V_SN = 0, 8, 16, 24, 32, 56, 184, 216
NV = 232


class Arena:
    def __init__(self, sb, size):
        self.sb = sb
        self.size = size
        self.top = 0

    def f32(self, n):
        n = (n + 1) // 2 * 2
        a = self.top
        self.top += n
        assert self.top <= self.size, ("SBUF arena overflow", self.top, self.size)
        return self.sb[:, a:a + n]

    def bf16(self, n):
        return self.f32((n + 1) // 2).bitcast(BF16)[:, 0:n]

    def mark(self):
        return self.top

    def release(self, m):
        self.top = m


def build(stage=3):
    nc = bass.Bass("TRN2", target_bir_lowering=False)

    def dram(name, shape, kind="ExternalInput"):
        return nc.dram_tensor(name, shape, F32, kind=kind).ap()

    x_d = dram("x", [T, D])
    vecs_d = dram("vecs", [128, NV])
    rep_d = dram("rep", [128, 96])
    f1wi_d = dram("ffn1_w_in", [D, 2 * FF])
    f1wo_d = dram("ffn1_w_out", [FF, D])
    win_d = dram("w_in", [D, NIN])
    swo_d = dram("short_w_out", [D, D])
    sso_d = dram("ssm_w_out", [2 * D, D])
    wo_d = dram("w_out", [D, D])
    f2wi_d = dram("ffn2_w_in", [D, 2 * FF])
    f2wo_d = dram("ffn2_w_out", [FF, D])
    out_d = dram("out", [T, D], kind="ExternalOutput")

    es = ExitStack()
    ARENA = 52700
    sb = es.enter_context(nc.sbuf_tensor("sb", [128, ARENA], F32))
    ps = [es.enter_context(nc.psum_tensor("ps%d" % i, [128, 512], F32)) for i in range(8)]
    P = Prog(nc)
    A = Arena(sb, ARENA)

    def kview(ap, k):
        return ap.rearrange("p (k t) -> p k t", k=k)

    xT = kview(A.f32(KD * T), KD)
    vecs = A.f32(NV)
    rep = A.f32(96)
    ident = A.f32(128)
    identb = A.bf16(128)
    onesDb = A.bf16(128)
    ones = A.f32(128)
    Um = A.f32(128)
    SLm = A.f32(128)
    Umb = A.bf16(128)
    SLmb = A.bf16(128)
    Abc = A.f32(32)
    wdt = kview(A.bf16(KD * 32), KD)
    S32 = kview(A.f32(8 * 256), 8)
    Sbf = kview(A.bf16(8 * 256), 8)
    uhalo = kview(A.f32(8 * 2), 8)
    phalo = kview(A.f32(32 * 3), 32)
    cwh = A.f32(128)
    cbh = A.f32(32)
    nhalf = A.f32(2)[:, 0:1]
    dtb = rep[:, 0:32]
    Dbc = rep[:, 64:96]

    def dve(fn, reads, writes):
        return P.op("dve", fn, reads=reads, writes=writes)

    def act(fn, reads, writes):
        return P.op("act", fn, reads=reads, writes=writes)

    def pool(fn, reads, writes):
        return P.op("pool", fn, reads=reads, writes=writes)

    def pe(fn, reads, writes):
        return P.op("pe", fn, reads=reads, writes=writes)

    def mm(out, lhsT, rhs, start, stop):
        return pe(lambda e: e.matmul(out, lhsT=lhsT, rhs=rhs, start=start, stop=stop),
                  [lhsT, rhs], [out])

    def tr(out, in_, idn):
        return pe(lambda e: e.transpose(out=out, in_=in_, identity=idn), [in_, idn], [out])

    def wload(dst, src):
        return P.dma("pool", dst, src, writes=[dst])

    P.dma("sp", vecs, vecs_d, writes=[vecs])
    P.dma("sp", rep, rep_d, writes=[rep])
    pool(lambda e: e.memset(ones, 1.0), [], [ones])
    pool(lambda e: e.memset(nhalf, -0.5), [], [nhalf])
    pool(lambda e: e.memset(onesDb, 1.0 / D), [], [onesDb])
    pool(lambda e: e.memset(ident, 1.0), [], [ident])
    pool(lambda e: e.affine_select(out=ident, in_=ident, pattern=[[1, 128]], compare_op=ALU.is_equal,
                                   fill=0.0, base=0, channel_multiplier=-1), [ident], [ident])
    pool(lambda e: e.tensor_copy(out=identb, in_=ident), [ident], [identb])
    pool(lambda e: e.memset(Um, 1.0), [], [Um])
    pool(lambda e: e.affine_select(out=Um, in_=Um, pattern=[[1, 128]], compare_op=ALU.is_ge,
                                   fill=0.0, base=0, channel_multiplier=-1), [Um], [Um])
    pool(lambda e: e.memset(SLm, 1.0), [], [SLm])
    pool(lambda e: e.affine_select(out=SLm, in_=SLm, pattern=[[-1, 128]], compare_op=ALU.is_gt,
                                   fill=0.0, base=0, channel_multiplier=1), [SLm], [SLm])
    pool(lambda e: e.tensor_copy(out=Umb, in_=Um), [Um], [Umb])
    pool(lambda e: e.tensor_copy(out=SLmb, in_=SLm), [SLm], [SLmb])
    pool(lambda e: e.memset(S32.rearrange("p g c -> p (g c)"), 0.0), [], [S32])
    pool(lambda e: e.memset(Sbf.rearrange("p g c -> p (g c)"), 0.0), [], [Sbf])
    pool(lambda e: e.memset(uhalo.rearrange("p g c -> p (g c)"), 0.0), [], [uhalo])
    pool(lambda e: e.memset(phalo.rearrange("p g c -> p (g c)"), 0.0), [], [phalo])
    act(lambda e: e.activation(out=Abc, in_=rep[:, 32:64], func=ACTF.Exp), [rep], [Abc])
    dve(lambda e: e.tensor_scalar(out=Abc, in0=Abc, scalar1=-1.0, scalar2=None, op0=ALU.mult), [Abc], [Abc])
    dve(lambda e: e.tensor_scalar(out=cwh, in0=vecs[:, V_CW:V_CW + 128], scalar1=0.5, scalar2=None, op0=ALU.mult), [vecs], [cwh])
    dve(lambda e: e.tensor_scalar(out=cbh, in0=vecs[:, V_CB:V_CB + 32], scalar1=0.5, scalar2=None, op0=ALU.mult), [vecs], [cbh])
    wload(wdt, win_d.rearrange("(k p) c -> p k c", p=128)[:, :, C_DT:C_DT + 32])

    m0 = A.mark()
    xin = [A.f32(D) for _ in range(2)]
    for tt in range(T // 128):
        xi = xin[tt % 2]
        P.dma("sp", xi, x_d[tt * 128:(tt + 1) * 128, :], writes=[xi])
        for b in range(2):
            bank = ps[(tt * 2 + b) % 4]
            for q in range(4):
                k = b * 4 + q
                tr(bank[:, q * 128:(q + 1) * 128], xi[:, k * 128:(k + 1) * 128], ident)
            dst = xT[:, b * 4:(b + 1) * 4, tt * 128:(tt + 1) * 128]
            src = kview(bank[:, :], 4)
            if b == 0:
                act(lambda e, dst=dst, src=src: e.copy(out=dst, in_=src), [src], [dst])
            else:
                dve(lambda e, dst=dst, src=src: e.tensor_copy(out=dst, in_=src), [src], [dst])
    A.release(m0)

    def rmsnorm(t0, ntb, wcol, dst_fn, tmp):
        sq, rt, rstd = tmp
        for tb in range(ntb):
            ts = slice(t0 + tb * 512, t0 + (tb + 1) * 512)
            msp = ps[6 + tb % 2][:, :]
            for k in range(KD):
                s = sq[k % 2]
                act(lambda e, s=s, k=k, ts=ts: e.activation(out=s, in_=xT[:, k, ts], func=ACTF.Square),
                    [xT[:, k, ts]], [s])
                mm(msp, onesDb, s, k == 0, k == KD - 1)
            act(lambda e, msp=msp: e.activation(out=rt, in_=msp, func=ACTF.Ln, bias=EPS, scale=1.0),
                [msp], [rt])
            act(lambda e: e.activation(out=rstd, in_=rt, func=ACTF.Exp, scale=-0.5), [rt], [rstd])
            for k in range(KD):
                dst = dst_fn(k, tb)
                dve(lambda e, dst=dst, k=k, ts=ts: e.scalar_tensor_tensor(
                    out=dst, in0=xT[:, k, ts], scalar=vecs[:, wcol + k:wcol + k + 1], in1=rstd,
                    op0=ALU.mult, op1=ALU.mult), [xT[:, k, ts], vecs, rstd], [dst])

    def norm_tmps():
        return ([A.bf16(512), A.bf16(512)], A.f32(512), A.f32(512))

    def ffn(wcol, wi_d, wo_d_):
        m = A.mark()
        hT = kview(A.bf16(KD * T), KD)
        tmp = norm_tmps()
        rmsnorm(0, 4, wcol, lambda k, tb: hT[:, k, tb * 512:(tb + 1) * 512], tmp)
        pieces = [(0, 4), (4, 4), (8, 4), (12, 4), (16, 3), (19, 3)]
        actb = [A.bf16(4 * T).rearrange("p (j t) -> p j t", j=4) for _ in range(2)]
        wi = [kview(A.bf16(KD * 256), KD) for _ in range(3)]
        wo = [A.bf16(4 * D).rearrange("p (j c) -> p j c", j=4) for _ in range(2)]
        sg = [A.f32(512) for _ in range(2)]
        wi_v = wi_d.rearrange("(k p) c -> p k c", p=128)

        def load_wi(j):
            w = wi[j % 3]
            wload(w[:, :, 0:128], wi_v[:, :, j * 128:(j + 1) * 128])
            wload(w[:, :, 128:256], wi_v[:, :, FF + j * 128:FF + (j + 1) * 128])

        def load_wo(pi):
            c0, n = pieces[pi]
            w = wo[pi % 2]
            wload(w[:, 0:n, :], wo_d_[c0 * 128:(c0 + n) * 128, :].rearrange("(j p) c -> p j c", p=128))

        def win(pi):
            c0, n = pieces[pi]
            ab = actb[pi % 2]
            for jj in range(n):
                j = c0 + jj
                if j + 2 < NF:
                    load_wi(j + 2)
                w = wi[j % 3]
                for tb in range(4):
                    ts = slice(tb * 512, (tb + 1) * 512)
                    G = ps[(j * 4 + tb) % 2][:, :]
                    U = ps[2 + (j * 4 + tb) % 2][:, :]
                    for k in range(KD):
                        mm(G, w[:, k, 0:128], hT[:, k, ts], k == 0, k == KD - 1)
                    for k in range(KD):
                        mm(U, w[:, k, 128:256], hT[:, k, ts], k == 0, k == KD - 1)
                    s = sg[(j * 4 + tb) % 2]
                    act(lambda e, s=s, G=G: e.activation(out=s, in_=G, func=ACTF.Silu), [G], [s])
                    dst = ab[:, jj, ts]
                    dve(lambda e, dst=dst, s=s, U=U: e.tensor_tensor(out=dst, in0=U, in1=s, op=ALU.mult),
                        [U, s], [dst])

        def wout(pi):
            c0, n = pieces[pi]
            ab = actb[pi % 2]
            w = wo[pi % 2]
            last = (pi == len(pieces) - 1)
            order = [(dm, tb) for tb in range(4) for dm in range(KD)] if last else \
                    [(dm, tb) for dm in range(KD) for tb in range(4)]
            for oi, (dm, tb) in enumerate(order):
                if True:
                    ts = slice(tb * 512, (tb + 1) * 512)
                    Y = ps[4 + oi % 2][:, :]
                    for jj in range(n):
                        mm(Y, w[:, jj, dm * 128:(dm + 1) * 128], ab[:, jj, ts], jj == 0, jj == n - 1)
                    dst = xT[:, dm, ts]
                    dve(lambda e, dst=dst, Y=Y: e.scalar_tensor_tensor(out=dst, in0=Y, scalar=0.5, in1=dst,
                                                                      op0=ALU.mult, op1=ALU.add),
                        [Y, dst], [dst])

        load_wi(0)
        load_wi(1)
        load_wo(0)
        win(0)
        for pi in range(1, len(pieces)):
            load_wo(pi) if pi >= 1 else None
            win(pi)
            wout(pi - 1)
        wout(len(pieces) - 1)
        A.release(m)

    if stage >= 1:
        ffn(V_N1, f1wi_d, f1wo_d)

    def mixer():
        m = A.mark()
        TH = 1024
        win_v = win_d.rearrange("(k p) c -> p k c", p=128)
        hT = kview(A.bf16(KD * TH), KD)
        ynT = kview(A.bf16(16 * TH), 16)
        dtw = A.f32(512).rearrange("p (c w h) -> p c w h", c=8, w=2)
        dt_tm = dtw[:, :, 0, :]
        wst = dtw[:, :, 1, :]
        a_tm = kview(A.f32(256), 8)
        ahl = A.bf16(512).rearrange("p (c w h) -> p c w h", c=8, w=2)
        eall = kview(A.f32(768), 8)
        m_half = A.mark()
        for hh in range(2):
            A.release(m_half)
            t0 = hh * TH
            ntmp = norm_tmps()
            rmsnorm(t0, 2, V_NM, lambda k, tb: hT[:, k, tb * 512:(tb + 1) * 512], ntmp)
            A.release(m_half)
            dtp = ps[3][:, 0:256]
            for c in range(8):
                for k in range(KD):
                    mm(dtp[:, c * 32:(c + 1) * 32], hT[:, k, c * 128:(c + 1) * 128], wdt[:, k, :], k == 0, k == KD - 1)
            dtp3 = kview(dtp, 8)
            dtb_b = dtb.unsqueeze(1).to_broadcast([128, 8, 32])
            dve(lambda e: e.tensor_tensor(out=dt_tm, in0=dtp3, in1=dtb_b, op=ALU.add), [dtp, rep], [dt_tm])
            act(lambda e: e.activation(out=dt_tm, in_=dt_tm, func=ACTF.Exp), [dt_tm], [dt_tm])
            act(lambda e: e.activation(out=dt_tm, in_=dt_tm, func=ACTF.Ln, bias=1.0, scale=1.0), [dt_tm], [dt_tm])
            A_b = Abc.unsqueeze(1).to_broadcast([128, 8, 32])
            dve(lambda e: e.tensor_tensor(out=a_tm, in0=dt_tm, in1=A_b, op=ALU.mult), [dt_tm, Abc], [a_tm])
            dve(lambda e: e.tensor_copy(out=ahl[:, :, 0, :], in_=a_tm), [a_tm], [ahl[:, :, 0, :]])
            dve(lambda e: e.tensor_tensor(out=ahl[:, :, 1, :], in0=a_tm, in1=ahl[:, :, 0, :], op=ALU.subtract),
                [a_tm, ahl[:, :, 0, :]], [ahl[:, :, 1, :]])
            for c in range(8):
                bank = ps[4 + c // 4]
                base = (c % 4) * 96
                mm(bank[:, base:base + 32], Um, a_tm[:, c, :], True, True)
                mm(bank[:, base + 32:base + 64], SLm, a_tm[:, c, :], True, True)
                mm(bank[:, base + 64:base + 96], ones, a_tm[:, c, :], True, True)
            for b2 in range(2):
                src = ps[4 + b2][:, 0:384]
                dst = eall[:, b2 * 4:(b2 + 1) * 4, :].rearrange("p c e -> p (c e)") if False else None
                dstv = eall[:, b2 * 4:(b2 + 1) * 4, :]
                srcv = kview(src, 4)
                act(lambda e, dstv=dstv, srcv=srcv: e.activation(out=dstv, in_=srcv, func=ACTF.Exp), [src], [dstv])
            dve(lambda e: e.tensor_tensor(out=wst, in0=dt_tm, in1=eall[:, :, 32:64], op=ALU.mult),
                [dt_tm, eall], [wst])

            mg = A.mark()
            wproj = kview(A.bf16(KD * 512), KD)
            wz = [kview(A.bf16(KD * 256), KD) for _ in range(2)]
            pre = [A.bf16(4 + TH)[:, 0:3 + TH] for _ in range(2)]
            acc = A.f32(TH)
            thb = A.f32(TH)
            dg2 = [kview(A.bf16(4 * 128), 4) for _ in range(2)]
            fm2 = [kview(A.bf16(4 * TH), 4) for _ in range(2)]
            dmat2 = [kview(A.bf16(4 * 128), 4)] * 2
            xb_tm = [A.bf16(384) for _ in range(2)]
            xd2 = [A.bf16(512) for _ in range(2)]
            xdt = [t_[:, 0:256] for t_ in xd2]
            xdtd = [t_[:, 256:512] for t_ in xd2]
            lseg = [A.bf16(1024).rearrange("p (w h s) -> p w h s", w=2, h=4) for _ in range(2)]
            Lt = [kview(A.f32(512), 4) for _ in range(2)]
            Mp = [kview(A.bf16(512), 4) for _ in range(2)]
            CBm = [A.f32(128) for _ in range(2)]
            sz = [A.f32(256) for _ in range(2)]
            t1 = A.f32(256)
            yv = A.f32(256)
            gyx = A.f32(258)
            gy = gyx[:, 0:256]
            yn = [A.bf16(256) for _ in range(2)]
            junk = A.f32(258)[:, 0:257]
            pool(lambda e: e.memset(gyx[:, 256:257], 16.0 * (4.0 * EPS) ** 0.5), [], [gyx[:, 256:257]])
            ssb = A.f32(2)
            rsb = A.f32(2)

            def load_wproj(g):
                wload(wproj[:, :, 0:256], win_v[:, :, C_XS + 256 * g:C_XS + 256 * (g + 1)])
                wload(wproj[:, :, 256:384], win_v[:, :, C_BM + 128 * g:C_BM + 128 * (g + 1)])
                wload(wproj[:, :, 384:512], win_v[:, :, C_CM + 128 * g:C_CM + 128 * (g + 1)])

            def load_wz(g):
                wload(wz[g % 2], win_v[:, :, C_Z + 256 * g:C_Z + 256 * (g + 1)])

            def GF(g, pbank):
                fm = fm2[g % 2]
                cbank = tuple(b_ + 4 if b_ < 4 else b_ - 4 for b_ in pbank)
                pieces = []
                for q in range(4):
                    ch = (2 * g + q) if q < 2 else (16 + g if q == 2 else 24 + g)
                    pr = pre[q % 2]
                    dg = dg2[q % 2]

                    def pa0(pr=pr, ch=ch):
                        dve(lambda e: e.tensor_copy(out=pr[:, 0:3], in_=phalo[:, ch, :]), [phalo[:, ch, :]], [pr[:, 0:3]])

                    def pa_tb(tb, pr=pr, q=q):
                        bank = ps[pbank[tb]][:, :]
                        for k in range(KD):
                            mm(bank, wproj[:, k, q * 128:(q + 1) * 128], hT[:, k, tb * 512:(tb + 1) * 512], k == 0, k == KD - 1)
                        dst = pr[:, 3 + tb * 512:3 + (tb + 1) * 512]
                        act(lambda e: e.copy(out=dst, in_=bank), [bank], [dst])

                    def pa2(pr=pr, ch=ch):
                        dve(lambda e: e.tensor_copy(out=phalo[:, ch, :], in_=pr[:, TH:TH + 3]), [pr[:, TH:TH + 3]], [phalo[:, ch, :]])

                    def pdg(dg=dg, ch=ch):
                        dve(lambda e: e.tensor_tensor(
                            out=dg, in0=identb.unsqueeze(1).to_broadcast([128, 4, 128]),
                            in1=cwh[:, 4 * ch:4 * ch + 4].unsqueeze(2).to_broadcast([128, 4, 128]), op=ALU.mult),
                            [identb, cwh], [dg])

                    def pc_tb(tb, pr=pr, dg=dg, ch=ch):
                        bank = ps[cbank[tb]][:, :]
                        for j in range(4):
                            mm(bank, dg[:, j, :], pr[:, j + tb * 512:j + (tb + 1) * 512], j == 0, j == 3)
                        dst = acc[:, tb * 512:(tb + 1) * 512]
                        act(lambda e: e.activation(out=dst, in_=bank, func=ACTF.Identity, bias=cbh[:, ch:ch + 1], scale=1.0),
                            [bank, cbh], [dst])

                    def pb4():
                        act(lambda e: e.activation(out=thb, in_=acc, func=ACTF.Tanh, scale=1.0), [acc], [thb])

                    def pb6(q=q):
                        dve(lambda e: e.scalar_tensor_tensor(out=fm[:, q, :], in0=thb, scalar=1.0, in1=acc,
                                                             op0=ALU.add, op1=ALU.mult),
                            [acc, thb], [fm[:, q, :]])

                    pieces.append([pa0, lambda f=pa_tb: f(0), lambda f=pa_tb: f(1), pa2, pdg])
                    pieces.append([lambda f=pc_tb: f(0), lambda f=pc_tb: f(1), pb4, pb6])
                return pieces

            def S1(it, t):
                g, c = it
                p = t % 2
                fm = fm2[g % 2]
                tc = slice(c * 128, (c + 1) * 128)
                bA, bB, bC = ps[4 * p], ps[4 * p + 1], ps[4 * p + 2]
                trb = bA[:, :].bitcast(BF16)
                xb = xb_tm[p]
                x_tm = xb[:, 0:256]
                x3 = kview(x_tm, 4)
                cbp = bC[:, 256:384]
                zp = bC[:, 0:256]
                cb = CBm[p]
                ls = lseg[p]
                L = Lt[p]
                M = Mp[p]
                szc = sz[p]
                w = wz[g % 2]

                def k0():
                    tr(trb[:, 0:128], fm[:, 0, tc], identb)
                    tr(trb[:, 128:256], fm[:, 1, tc], identb)
                    tr(trb[:, 256:384], fm[:, 2, tc], identb)
                    a_b = ahl[:, c, :, 4 * g:4 * g + 4].unsqueeze(3).to_broadcast([128, 2, 4, 128])
                    m_b = SLmb.unsqueeze(1).unsqueeze(1).to_broadcast([128, 2, 4, 128])
                    pool(lambda e: e.tensor_tensor(out=ls, in0=m_b, in1=a_b, op=ALU.mult), [SLmb, ahl[:, c, :, :]], [ls])

                def k1():
                    act(lambda e: e.copy(out=xb, in_=trb[:, 0:384]), [trb], [xb])
                    mm(cbp, fm[:, 2, tc], fm[:, 3, tc], True, True)

                def k2():
                    dve(lambda e: e.tensor_tensor(out=cb, in0=cbp, in1=Um, op=ALU.mult), [cbp, Um], [cb])
                    dw_b = dtw[:, c, :, 4 * g:4 * g + 4].unsqueeze(3).to_broadcast([128, 2, 4, 64])
                    x_b = x3.unsqueeze(1).to_broadcast([128, 2, 4, 64])
                    xo = xd2[p].rearrange("p (w h e) -> p w h e", w=2, h=4)
                    pool(lambda e: e.tensor_tensor(out=xo, in0=x_b, in1=dw_b, op=ALU.mult), [x_tm, dtw[:, c, :, :]], [xd2[p]])
                    for h in range(4):
                        mm(bB[:, h * 128:(h + 1) * 128], ls[:, 0, h, :], Umb, True, False)
                        mm(bB[:, h * 128:(h + 1) * 128], ls[:, 1, h, :], Umb, False, True)

                def k3():
                    act(lambda e: e.activation(out=L, in_=kview(bB[:, :], 4), func=ACTF.Exp), [bB[:, :]], [L])
                    for k in range(KD):
                        mm(zp, hT[:, k, tc], w[:, k, :], k == 0, k == KD - 1)

                def k4():
                    dve(lambda e: e.tensor_tensor(out=M, in0=L, in1=cb.unsqueeze(1).to_broadcast([128, 4, 128]),
                                                  op=ALU.mult), [L, cb], [M])
                    act(lambda e: e.activation(out=szc, in_=zp, func=ACTF.Tanh, scale=0.5), [zp], [szc])

                def k5():
                    dve(lambda e: e.scalar_tensor_tensor(out=szc, in0=szc, scalar=1.0, in1=zp, op0=ALU.add, op1=ALU.mult),
                        [szc, zp], [szc])

                return [k0, k1, k2, k3, k4, k5]

            def S2(it, t):
                g, c = it
                p = t % 2
                fm = fm2[g % 2]
                dmat = dmat2[g % 2]
                tc = slice(c * 128, (c + 1) * 128)
                bA, bD = ps[4 * p], ps[4 * p + 3]
                xb = xb_tm[p]
                x_tm = xb[:, 0:256]
                B_tm = xb[:, 256:384]
                xd = xdt[p]
                xdd = xdtd[p]
                M = Mp[p]
                szc = sz[p]
                yp = bD[:, 0:256]
                yop = bD[:, 256:512]
                stp = bA[:, 192:448]
                Sg = S32[:, g, :]
                ss = ssb[:, p:p + 1]
                rs = rsb[:, p:p + 1]
                ynn = yn[p]

                def k0():
                    for h in range(4):
                        hs = slice(h * 64, (h + 1) * 64)
                        mm(yp[:, hs], M[:, h, :], xd[:, hs], True, False)
                        mm(yp[:, hs], dmat[:, h, :], x_tm[:, hs], False, True)
                    mm(yop, fm[:, 3, tc], Sbf[:, g, :], True, True)
                    mm(stp, B_tm, xdd, True, True)
                    et_b = eall[:, c, 64 + 4 * g:64 + 4 * g + 4].unsqueeze(2).to_broadcast([128, 4, 64])
                    pool(lambda e: e.tensor_tensor(out=kview(Sg, 4), in0=kview(Sg, 4), in1=et_b, op=ALU.mult),
                         [Sg, eall], [Sg])

                def k1():
                    ecs_b = eall[:, c, 4 * g:4 * g + 4].unsqueeze(2).to_broadcast([128, 4, 64])
                    dve(lambda e: e.tensor_tensor(out=kview(t1, 4), in0=kview(yop, 4), in1=ecs_b, op=ALU.mult),
                        [yop, eall], [t1])

                def k2():
                    dve(lambda e: e.tensor_tensor(out=yv, in0=yp, in1=t1, op=ALU.add), [yp, t1], [yv])
                    dve(lambda e: e.tensor_tensor(out=Sg, in0=stp, in1=Sg, op=ALU.add), [stp, Sg], [Sg])

                def k3():
                    dve(lambda e: e.tensor_tensor(out=gy, in0=yv, in1=szc, op=ALU.mult), [yv, szc], [gy])
                    act(lambda e: e.copy(out=Sbf[:, g, :], in_=Sg), [Sg], [Sbf[:, g, :]])

                def k4():
                    act(lambda e: e.activation(out=junk, in_=gyx[:, 0:257], func=ACTF.Square, scale=1.0 / 16, accum_out=ss),
                        [gyx[:, 0:257]], [junk, ss])

                def k5():
                    pass

                def k6():
                    pool(lambda e: e.tensor_tensor(out=rs, in0=ss, in1=nhalf, op=ALU.pow), [ss, nhalf], [rs])

                def k7():
                    act(lambda e: e.activation(out=ynn, in_=gy, func=ACTF.Copy, scale=rs), [gy, rs], [ynn])

                return [k0, k1, k2, k3, k4, k5, k6, k7]

            def S3(it, t):
                g, c = it
                p = t % 2
                tc = slice(c * 128, (c + 1) * 128)
                ytp = ps[4 * p + 3][:, 0:128].bitcast(BF16)
                ynn = yn[p]

                def k0():
                    for i in range(2):
                        tr(ytp[:, i * 128:(i + 1) * 128], ynn[:, i * 128:(i + 1) * 128], identb)

                def k1():
                    for i in range(2):
                        dst = ynT[:, 2 * g + i, tc]
                        sc = vecs[:, V_SN + 2 * g + i:V_SN + 2 * g + i + 1]
                        act(lambda e, dst=dst, i=i, sc=sc: e.activation(out=dst, in_=ytp[:, i * 128:(i + 1) * 128],
                                                                       func=ACTF.Copy, scale=sc), [ytp, vecs], [dst])

                return [k0, k1]

            def interleave(lists):
                n = max(len(l) for l in lists)
                for k in range(n):
                    for l in lists:
                        if k < len(l):
                            l[k]()

            def build_dmat(g):
                dm_ = dmat2[g % 2]
                pool(lambda e: e.tensor_tensor(
                    out=dm_, in0=ident.unsqueeze(1).to_broadcast([128, 4, 128]),
                    in1=Dbc[:, 4 * g:4 * g + 4].unsqueeze(2).to_broadcast([128, 4, 128]), op=ALU.mult),
                    [ident, rep], [dm_])

            iters = [(g, c) for g in range(8) for c in range(8)]
            NI = len(iters)
            load_wproj(0)
            load_wz(0)
            for piece in GF(0, (1, 2)):
                for th in piece:
                    th()
            nextgf = None
            for t in range(NI + 2):
                lists = []
                if t < NI:
                    g, c = iters[t]
                    if c == 0:
                        build_dmat(g)
                        if g + 1 < 8:
                            load_wproj(g + 1)
                            load_wz(g + 1)
                            p = t % 2
                            nextgf = GF(g + 1, (4 * (1 - p) + 1, 4 * (1 - p) + 2))
                        else:
                            nextgf = None
                    lists.append(S1(iters[t], t))
                if 0 <= t - 1 < NI:
                    lists.append(S2(iters[t - 1], t - 1))
                if 0 <= t - 2 < NI:
                    lists.append(S3(iters[t - 2], t - 2))
                if t < NI and nextgf is not None:
                    lists.append(nextgf[iters[t][1]])
                interleave(lists)
            A.release(mg)
            ybg = kview(A.bf16(KD * TH), KD)
            mg = A.mark()

            wso = [kview(A.bf16(16 * 128), 16) for _ in range(2)]
            wgb = [kview(A.bf16(KD * 128), KD) for _ in range(2)]
            sgt = [A.f32(512) for _ in range(2)]
            sso_v = sso_d.rearrange("(k p) c -> p k c", p=128)

            def load_b(dm):
                wload(wso[dm % 2], sso_v[:, :, dm * 128:(dm + 1) * 128])
                wload(wgb[dm % 2], win_v[:, :, C_GB + dm * 128:C_GB + (dm + 1) * 128])

            load_b(0)
            for dm in range(KD):
                if dm + 1 < KD:
                    load_b(dm + 1)
                for tb in range(2):
                    ts = slice(tb * 512, (tb + 1) * 512)
                    yb = ps[(dm * 2 + tb) % 2][:, :]
                    gp = ps[2 + (dm * 2 + tb) % 2][:, :]
                    for c16 in range(16):
                        mm(yb, wso[dm % 2][:, c16, :], ynT[:, c16, ts], c16 == 0, c16 == 15)
                    for k in range(KD):
                        mm(gp, wgb[dm % 2][:, k, :], hT[:, k, ts], k == 0, k == KD - 1)
                    s = sgt[(dm * 2 + tb) % 2]
                    act(lambda e, s=s, gp=gp: e.activation(out=s, in_=gp, func=ACTF.Sigmoid), [gp], [s])
                    dst = ybg[:, dm, ts]
                    dve(lambda e, dst=dst, yb=yb, s=s: e.tensor_tensor(out=dst, in0=yb, in1=s, op=ALU.mult), [yb, s], [dst])
            A.release(mg)

            ya_in = ynT[:, 0:8, :]
            ma = A.mark()
            wa3 = [kview(A.bf16(KD * 384), KD) for _ in range(2)]
            ub = [A.f32(2 + TH) for _ in range(2)]
            va = [A.f32(TH) for _ in range(2)]
            cj = [A.f32(512) for _ in range(2)]
            bj = [A.f32(TH) for _ in range(2)]

            def load_a(j):
                w = wa3[j % 2]
                wload(w[:, :, 0:128], win_v[:, :, C_C + j * 128:C_C + (j + 1) * 128])
                wload(w[:, :, 128:256], win_v[:, :, C_XA + j * 128:C_XA + (j + 1) * 128])
                wload(w[:, :, 256:384], win_v[:, :, C_B + j * 128:C_B + (j + 1) * 128])

            load_a(0)
            for j in range(8):
                if j + 1 < 8:
                    load_a(j + 1)
                w = wa3[j % 2]
                u = ub[j % 2]
                v = va[j % 2]
                b_ = bj[j % 2]
                pool(lambda e, u=u, j=j: e.tensor_copy(out=u[:, 0:2], in_=uhalo[:, j, :]), [uhalo[:, j, :]], [u[:, 0:2]])
                for tb in range(2):
                    ts = slice(tb * 512, (tb + 1) * 512)
                    cp = ps[(j * 2 + tb) % 2][:, :]
                    xp = ps[2 + (j * 2 + tb) % 2][:, :]
                    bp = ps[4 + (j * 2 + tb) % 2][:, :]
                    for k in range(KD):
                        mm(cp, w[:, k, 0:128], hT[:, k, ts], k == 0, k == KD - 1)
                    for k in range(KD):
                        mm(xp, w[:, k, 128:256], hT[:, k, ts], k == 0, k == KD - 1)
                    for k in range(KD):
                        mm(bp, w[:, k, 256:384], hT[:, k, ts], k == 0, k == KD - 1)
                    cc = cj[(j * 2 + tb) % 2]
                    act(lambda e, cc=cc, cp=cp: e.copy(out=cc, in_=cp), [cp], [cc])
                    dst = u[:, 2 + tb * 512:2 + (tb + 1) * 512]
                    dve(lambda e, dst=dst, xp=xp, cc=cc: e.tensor_tensor(out=dst, in0=xp, in1=cc, op=ALU.mult), [xp, cc], [dst])
                    bdst = b_[:, ts]
                    act(lambda e, bdst=bdst, bp=bp: e.copy(out=bdst, in_=bp), [bp], [bdst])
                pool(lambda e, u=u, j=j: e.tensor_copy(out=uhalo[:, j, :], in_=u[:, TH:TH + 2]), [u[:, TH:TH + 2]], [uhalo[:, j, :]])
                cw = V_SCW + 3 * j
                dve(lambda e, v=v, u=u, cw=cw: e.tensor_scalar(out=v, in0=u[:, 0:TH], scalar1=vecs[:, cw:cw + 1], scalar2=None,
                                                               op0=ALU.mult), [u, vecs], [v])
                for tpp in range(1, 3):
                    dve(lambda e, v=v, u=u, cw=cw, tpp=tpp: e.scalar_tensor_tensor(
                        out=v, in0=u[:, tpp:tpp + TH], scalar=vecs[:, cw + tpp:cw + tpp + 1], in1=v,
                        op0=ALU.mult, op1=ALU.add), [u, vecs, v], [v])
                dve(lambda e, j=j, v=v, b_=b_: e.tensor_tensor(out=ya_in[:, j, :], in0=v, in1=b_, op=ALU.mult),
                    [v, b_], [ya_in[:, j, :]])
            A.release(ma)
            wsa = [kview(A.bf16(KD * 128), KD) for _ in range(2)]
            wga = [kview(A.bf16(KD * 128), KD) for _ in range(2)]
            swo_v = swo_d.rearrange("(k p) c -> p k c", p=128)
            tg = [A.f32(512) for _ in range(2)]
            sgt = [A.f32(512) for _ in range(2)]

            def load_a2(dm):
                wload(wsa[dm % 2], swo_v[:, :, dm * 128:(dm + 1) * 128])
                wload(wga[dm % 2], win_v[:, :, C_GA + dm * 128:C_GA + (dm + 1) * 128])

            load_a2(0)
            for dm in range(KD):
                if dm + 1 < KD:
                    load_a2(dm + 1)
                for tb in range(2):
                    ts = slice(tb * 512, (tb + 1) * 512)
                    yap = ps[(dm * 2 + tb) % 2][:, :]
                    gp = ps[2 + (dm * 2 + tb) % 2][:, :]
                    for j in range(KD):
                        mm(yap, wsa[dm % 2][:, j, :], ya_in[:, j, ts], j == 0, j == KD - 1)
                    for k in range(KD):
                        mm(gp, wga[dm % 2][:, k, :], hT[:, k, ts], k == 0, k == KD - 1)
                    s = sgt[(dm * 2 + tb) % 2]
                    act(lambda e, s=s, gp=gp: e.activation(out=s, in_=gp, func=ACTF.Sigmoid), [gp], [s])
                    t_ = tg[(dm * 2 + tb) % 2]
                    dve(lambda e, t_=t_, yap=yap, s=s: e.tensor_tensor(out=t_, in0=yap, in1=s, op=ALU.mult), [yap, s], [t_])
                    dst = ybg[:, dm, ts]
                    dve(lambda e, dst=dst, t_=t_: e.tensor_tensor(out=dst, in0=t_, in1=dst, op=ALU.add), [t_, dst], [dst])
            wof = [kview(A.bf16(KD * 128), KD) for _ in range(2)]
            wo_v = wo_d.rearrange("(k p) c -> p k c", p=128)
            wload(wof[0], wo_v[:, :, 0:128])
            for dm in range(KD):
                if dm + 1 < KD:
                    wload(wof[(dm + 1) % 2], wo_v[:, :, (dm + 1) * 128:(dm + 2) * 128])
                for tb in range(2):
                    ts = slice(tb * 512, (tb + 1) * 512)
                    op_ = ps[4 + (dm * 2 + tb) % 2][:, :]
                    for k in range(KD):
                        mm(op_, wof[dm % 2][:, k, :], ybg[:, k, ts], k == 0, k == KD - 1)
                    dst = xT[:, dm, t0 + tb * 512:t0 + (tb + 1) * 512]
                    dve(lambda e, dst=dst, op_=op_: e.tensor_tensor(out=dst, in0=op_, in1=dst, op=ALU.add), [op_, dst], [dst])
        A.release(m)

    if stage >= 2:
        mixer()
    if stage >= 3:
        ffn(V_N2, f2wi_d, f2wo_d)

    m = A.mark()
    hf = [kview(A.f32(KD * 512), KD) for _ in range(1)]
    ntmp = norm_tmps()
    otm = [A.f32(D) for _ in range(2)]
    out_ops = []
    for tb in range(4):
        h = hf[0]
        rmsnorm(tb * 512, 1, V_NF, lambda k, _tb, h=h: h[:, k, :], ntmp)
        for t4 in range(4):
            tt = tb * 4 + t4
            o = otm[tt % 2]
            for b in range(2):
                bank = ps[(tt * 2 + b) % 4]
                for q in range(4):
                    k = b * 4 + q
                    tr(bank[:, q * 128:(q + 1) * 128], h[:, k, t4 * 128:(t4 + 1) * 128], ident)
                dst = o[:, b * 512:(b + 1) * 512]
                if b == 0:
                    act(lambda e, dst=dst, bank=bank: e.copy(out=dst, in_=bank[:, :]), [bank[:, :]], [dst])
                else:
                    dve(lambda e, dst=dst, bank=bank: e.tensor_copy(out=dst, in_=bank[:, :]), [bank[:, :]], [dst])
            out_ops.append(P.dma("sp", out_d[tt * 128:(tt + 1) * 128, :], o, reads=[o]))
    A.release(m)
    P.fence("sp", [o for o in P.ops if o.is_dma])
    if SCHED:
        P.schedule()
    P.emit(es)
    es.close()
    return nc, P


def _pack_inputs(inp):
    f = lambda a: np.ascontiguousarray(np.asarray(a, dtype=np.float32))
    col = lambda v, n: f(v).reshape(n, 128).T
    vecs = np.zeros((128, NV), np.float32)
    vecs[:, V_N1:V_N1 + 8] = col(inp["ffn1_norm"][0], 8)
    vecs[:, V_NM:V_NM + 8] = col(inp["mix_norm"][0], 8)
    vecs[:, V_N2:V_N2 + 8] = col(inp["ffn2_norm"][0], 8)
    vecs[:, V_NF:V_NF + 8] = col(inp["final_norm"], 8)
    scw = f(inp["short_conv_w"][0])
    vecs[:, V_SCW:V_SCW + 24] = scw.reshape(3, 8, 128).transpose(2, 1, 0).reshape(128, 24)
    cw = f(inp["ssm_conv_w"][0])
    vecs[:, V_CW:V_CW + 128] = cw.reshape(4, 32, 128).transpose(2, 1, 0).reshape(128, 128)
    vecs[:, V_CB:V_CB + 32] = col(inp["ssm_conv_b"][0], 32)
    vecs[:, V_SN:V_SN + 16] = col(inp["ssm_norm"][0], 16)
    rep = np.zeros((128, 96), np.float32)
    rep[:, 0:32] = np.broadcast_to(f(inp["ssm_dt_bias"][0]), (128, 32))
    rep[:, 32:64] = np.broadcast_to(f(inp["ssm_A_log"][0]), (128, 32))
    rep[:, 64:96] = np.broadcast_to(f(inp["ssm_D"][0]), (128, 32))
    shared = {"vecs": vecs, "rep": rep}
    for nm in ("ffn1_w_in", "ffn1_w_out", "w_in", "short_w_out", "ssm_w_out", "w_out", "ffn2_w_in", "ffn2_w_out"):
        shared[nm] = f(inp[nm][0])
    return shared


_NC_CACHE = {}


def kernel(**inputs):
    x = np.asarray(inputs["x"], dtype=np.float32)
    nb = x.shape[0]
    shared = _pack_inputs(inputs)
    if "nc" not in _NC_CACHE:
        _NC_CACHE["nc"] = build(3)[0]
    nc = _NC_CACHE["nc"]
    in_maps = []
    for b in range(nb):
        d = dict(shared)
        d["x"] = np.ascontiguousarray(x[b])
        in_maps.append(d)
    res = run_bass_kernel_spmd(nc, in_maps, core_ids=list(range(nb)))
    return np.stack([np.asarray(r["out"], dtype=np.float32) for r in res.results], axis=0)
```

```python
import numpy as np
import concourse.bass as bass
import concourse.mybir as mybir

F32 = mybir.dt.float32
BF16 = mybir.dt.bfloat16
ACTF = mybir.ActivationFunctionType
ALU = mybir.AluOpType

_DSZ = {F32: 4, BF16: 2, mybir.dt.float32r: 4, mybir.dt.int32: 4, mybir.dt.uint32: 4,
        mybir.dt.uint8: 1, mybir.dt.int8: 1, mybir.dt.uint16: 2, mybir.dt.int16: 2,
        mybir.dt.float16: 2}


def dsize(dt):
    return _DSZ[dt]


def footprint(ap):
    t = ap.tensor
    esz = dsize(ap.dtype)
    shape = list(t.shape)
    pstride = 1
    for s in shape[1:]:
        pstride *= int(s)
    tesz = dsize(t.dtype)
    pstride = pstride * tesz // esz
    aps = [(int(s), int(c)) for s, c in ap.ap]
    off = int(ap.offset)
    p0 = off // pstride
    foff = off % pstride
    pstep, pcnt = aps[0]
    if pcnt > 1:
        assert pstep == pstride, (pstep, pstride, ap)
    p1 = p0 + pcnt
    dims = [(s, c) for s, c in aps[1:] if c > 1]
    ivs = [(foff, foff + 1)]
    dims.sort(key=lambda sc: abs(sc[0]))
    for s, c in dims:
        s = abs(s)
        if s == 0:
            continue
        new = []
        lo0, hi0 = ivs[0][0], ivs[-1][1]
        width = hi0 - lo0
        if len(ivs) == 1 and s <= width:
            ivs = [(lo0, lo0 + s * (c - 1) + width)]
            continue
        if len(ivs) * c > 64:
            ivs = [(lo0, hi0 + s * (c - 1))]
            continue
        for i in range(c):
            for lo, hi in ivs:
                new.append((lo + i * s, hi + i * s))
        new.sort()
        ivs = new
    ivs = [(lo * esz, hi * esz) for lo, hi in ivs]
    return (t.name, p0, p1, ivs)


class Op:
    __slots__ = ("eng", "fn", "deps", "marked", "tok", "is_dma", "seq", "inc", "name", "soft", "cost", "nbytes")


class Prog:
    ENGS = ("pe", "act", "dve", "pool", "sp")

    def __init__(self, nc, n_dma_sems=20):
        self.nc = nc
        self.ops = []
        self.live = {}
        self.n_dma_sems = n_dma_sems
        self.dma_rr = {e: 0 for e in self.ENGS}
        self.dma_last = {}
        self.final_deps = []

    def _deps_for(self, o, ap, is_write, deps):
        name, p0, p1, ivs = footprint(ap)
        recs = self.live.setdefault(name, [])
        if name.startswith("ps"):
            for r in recs:
                if r[3] is not o:
                    self._add_dep(o, r[3], raw=(r[4] and not is_write), deps=deps)
            keep = [r for r in recs if r[3] is o]
            if not keep or is_write:
                keep = [(0, 128, [(0, 2048)], o, is_write or any(r[4] for r in keep))]
            self.live[name] = keep
            return
        lo_all, hi_all = ivs[0][0], ivs[-1][1]
        keep = []
        for r in recs:
            rp0, rp1, rivs, rop, rw = r
            if rop is o:
                keep.append(r)
                continue
            ov = False
            if rp0 < p1 and p0 < rp1 and rivs[0][0] < hi_all and lo_all < rivs[-1][1]:
                for lo, hi in ivs:
                    for rlo, rhi in rivs:
                        if rlo < hi and lo < rhi:
                            ov = True
                            break
                    if ov:
                        break
            if ov and (is_write or rw):
                self._add_dep(o, rop, raw=(rw and not is_write), deps=deps)
            if is_write and ov and rp0 >= p0 and rp1 <= p1:
                covered = True
                for rlo, rhi in rivs:
                    c1 = False
                    for lo, hi in ivs:
                        if lo <= rlo and rhi <= hi:
                            c1 = True
                            break
                    if not c1:
                        covered = False
                        break
                if covered:
                    continue
            keep.append(r)
        if not is_write:
            k2 = []
            for r in keep:
                if (not r[4]) and r[3].eng == o.eng and (not r[3].is_dma) and (not o.is_dma) \
                        and r[0] == p0 and r[1] == p1 and r[2] == ivs:
                    if r[3] is not o:
                        o.soft.add(r[3])
                    continue
                k2.append(r)
            keep = k2
        keep.append((p0, p1, ivs, o, is_write))
        self.live[name] = keep

    def _add_dep(self, o, d, raw, deps):
        if d is o:
            return
        if d.eng == o.eng and not d.is_dma and not o.is_dma:
            if o.eng == "pe":
                o.soft.add(d)
                return
        deps.add(d)

    def op(self, eng, fn, reads=(), writes=(), dma=False, name=None, cost=None):
        o = Op()
        o.eng = eng
        o.fn = fn
        o.is_dma = dma
        o.marked = False
        o.tok = None
        o.inc = 16 if dma else 1
        o.seq = len(self.ops)
        o.name = name
        o.soft = set()
        o.nbytes = 0
        o.cost = self._cost(eng, reads, writes, dma, o) if cost is None else cost
        deps = set()
        for ap in reads:
            self._deps_for(o, ap, False, deps)
        for ap in writes:
            self._deps_for(o, ap, True, deps)
        if dma:
            slot = self.dma_rr[eng] % self.n_dma_sems
            self.dma_rr[eng] += 1
            prev = self.dma_last.get((eng, slot))
            if prev is not None:
                deps.add(prev)
            self.dma_last[(eng, slot)] = o
            o.tok = ("dma", eng, slot)
            o.marked = True
        o.deps = sorted(deps, key=lambda d: d.seq)
        for d in o.deps:
            d.marked = True
        self.ops.append(o)
        return o

    @staticmethod
    def _fsize(ap):
        n = 1
        for d in ap.shape[1:]:
            n *= int(d)
        return n

    def _cost(self, eng, reads, writes, dma, o):
        aps = list(writes) if writes else list(reads)
        n = self._fsize(aps[0]) if aps else 1
        if dma:
            o.nbytes = n * int(aps[0].shape[0]) * 4
            return 1000.0 if eng == "pool" else 80.0
        if eng == "pe":
            mult = 4.0 if (reads and reads[0].dtype == F32) else 1.0
            return 10.0 + max(64, n) * mult / 2.15
        if eng == "act":
            return 220.0 + n / 1.4
        if eng == "dve":
            return (230.0 if (reads and reads[0].tensor.name.startswith("ps")) else 130.0) + n / 0.96
        if eng == "pool":
            return 520.0 + n * 1.3
        return 50.0

    def schedule(self):
        import heapq
        ops = self.ops
        n = len(ops)
        idx = {id(o): i for i, o in enumerate(ops)}
        succ = [[] for _ in range(n)]
        ndep = [0] * n
        for i, o in enumerate(ops):
            ds = set(o.deps) | o.soft
            ndep[i] = len(ds)
            for d in ds:
                succ[idx[id(d)]].append(i)
        plen = [0.0] * n
        for i in range(n - 1, -1, -1):
            m_ = 0.0
            for j in succ[i]:
                if plen[j] > m_:
                    m_ = plen[j]
            plen[i] = m_ + ops[i].cost + (2000.0 if ops[i].is_dma else 60.0)
        if PRIO_WINDOW:
            key = [(-(plen[i]) + PRIO_WINDOW * 0.0, i) for i in range(n)]
        rank = sorted(range(n), key=lambda i: (-plen[i], i)) if PRIO_CP else list(range(n))
        prio = [0] * n
        for r_, i in enumerate(rank):
            prio[i] = r_
        pend = {e: [] for e in self.ENGS}
        rdy = {e: [] for e in self.ENGS}
        free_at = {e: 0.0 for e in self.ENGS}
        start = [0.0] * n
        events = []
        for i, o in enumerate(ops):
            if ndep[i] == 0:
                heapq.heappush(rdy[o.eng], (prio[i], i))
        heapq.heappush(events, (0.0, -1))
        bus_free = 0.0
        done = 0
        BW = 160.0
        while events:
            T, ci = heapq.heappop(events)
            batch = [ci]
            while events and events[0][0] <= T:
                batch.append(heapq.heappop(events)[1])
            for c in batch:
                if c < 0:
                    continue
                for j in succ[c]:
                    ndep[j] -= 1
                    if ndep[j] == 0:
                        heapq.heappush(rdy[ops[j].eng], (prio[j], j))
            for e in self.ENGS:
                while free_at[e] <= T and rdy[e]:
                    i = heapq.heappop(rdy[e])[1]
                    o = ops[i]
                    start[i] = T
                    end = T + o.cost
                    free_at[e] = end
                    comp = end + 60.0
                    if o.is_dma:
                        b0 = max(end, bus_free)
                        bus_free = b0 + o.nbytes / BW
                        comp = bus_free + 1800.0
                    heapq.heappush(events, (comp, i))
                    if rdy[e]:
                        heapq.heappush(events, (end, -1))
                    done += 1
                    break
        assert done == n, (done, n)
        order = sorted(range(n), key=lambda i: (start[i], i))
        self.ops = [ops[i] for i in order]
        self.sim_time = max(start) if n else 0.0

    def dma(self, eng, out, in_, reads=(), writes=(), **kw):
        def fn(e, out=out, in_=in_, kw=kw):
            return e.dma_start(out=out, in_=in_, **kw)
        return self.op(eng, fn, reads=reads, writes=writes, dma=True)

    def emit(self, es):
        nc = self.nc
        sems = {e: es.enter_context(nc.semaphore("s_" + e)) for e in ("pe", "act", "dve", "pool")}
        dsem = {}
        for e in self.ENGS:
            if self.dma_rr[e] > 0:
                for s in range(min(self.n_dma_sems, self.dma_rr[e])):
                    dsem[(e, s)] = es.enter_context(nc.semaphore("d_%s_%d" % (e, s)))
        dcnt = {k: 0 for k in dsem}
        for o in self.ops:
            if o.is_dma:
                k = (o.tok[1], o.tok[2])
                dcnt[k] += 16
                o.tok = (dsem[k], dcnt[k])
        per = {e: [o for o in self.ops if o.eng == e] for e in self.ENGS}
        pos = {}
        for e_, lst in per.items():
            for i, o in enumerate(lst):
                pos[id(o)] = i
        need = {}
        marked = set()
        for e_ in self.ENGS:
            waited_pos = {}
            waited_dma = {}
            for o in per[e_]:
                best = {}
                w = []
                for d in o.deps:
                    if d.is_dma:
                        sem, val = d.tok
                        if waited_dma.get(id(sem), 0) < val:
                            waited_dma[id(sem)] = val
                            w.append(d)
                    else:
                        b = best.get(d.eng)
                        if b is None or pos[id(d)] > pos[id(b)]:
                            best[d.eng] = d
                for de, d in best.items():
                    if waited_pos.get(de, -1) < pos[id(d)]:
                        waited_pos[de] = pos[id(d)]
                        w.append(d)
                        marked.add(id(d))
                need[id(o)] = w
        cnt = {e: 0 for e in sems}
        for e_ in sems:
            for o in per[e_]:
                if (not o.is_dma) and id(o) in marked:
                    cnt[e_] += 1
                    o.tok = (sems[e_], cnt[e_])
        self.max_counts = dict(cnt)
        block = es.enter_context(nc.Block())

        def run(e, ename):
            for o in per[ename]:
                for d in need[id(o)]:
                    sem, val = d.tok
                    e.wait_ge(sem, val)
                if o.fn is None:
                    continue
                ins = o.fn(e)
                if o.is_dma or id(o) in marked:
                    ins.then_inc(o.tok[0], o.inc)

        if per["sp"]:
            @block.sync
            def _(e):
                run(e, "sp")
        if per["act"]:
            @block.scalar
            def _(e):
                run(e, "act")
        if per["dve"]:
            @block.vector
            def _(e):
                run(e, "dve")
        if per["pool"]:
            @block.gpsimd
            def _(e):
                run(e, "pool")
        if per["pe"]:
            @block.tensor
            def _(e):
                run(e, "pe")

    def fence(self, eng, ops):
        o = Op()
        o.eng = eng
        o.fn = None
        o.is_dma = False
        o.marked = False
        o.tok = None
        o.inc = 1
        o.seq = len(self.ops)
        o.name = "fence"
        o.soft = set()
        o.cost = 0.0
        o.nbytes = 0
        o.deps = sorted(set(ops), key=lambda d: d.seq)
        for d in o.deps:
            d.marked = True
        self.ops.append(o)
        return o


from contextlib import ExitStack
from concourse.bass_utils import run_bass_kernel_spmd

T = 2048
D = 1024
KD = 8
FF = 2816
NF = 22
NIN = 11296
EPS = 1e-5
SCHED = True
PRIO_WINDOW = 0
PRIO_CP = True
C_B, C_C, C_XA, C_Z, C_XS, C_BM, C_CM, C_DT, C_GA, C_GB = 0, 1024, 2048, 3072, 5120, 7168, 8192, 9216, 9248, 10272
V_N1, V_NM, V_N2, V_NF, V_SCW, V_CW, V_CB, V_SN = 0, 8, 16, 24, 32, 56, 184, 216
NV = 232


class Arena:
    def __init__(self, sb, size):
        self.sb = sb
        self.size = size
        self.top = 0

    def f32(self, n):
        n = (n + 1) // 2 * 2
        a = self.top
        self.top += n
        assert self.top <= self.size, ("SBUF arena overflow", self.top, self.size)
        return self.sb[:, a:a + n]

    def bf16(self, n):
        return self.f32((n + 1) // 2).bitcast(BF16)[:, 0:n]

    def mark(self):
        return self.top

    def release(self, m):
        self.top = m


def build(stage=3):
    nc = bass.Bass("TRN2", target_bir_lowering=False)

    def dram(name, shape, kind="ExternalInput"):
        return nc.dram_tensor(name, shape, F32, kind=kind).ap()

    x_d = dram("x", [T, D])
    vecs_d = dram("vecs", [128, NV])
    rep_d = dram("rep", [128, 96])
    f1wi_d = dram("ffn1_w_in", [D, 2 * FF])
    f1wo_d = dram("ffn1_w_out", [FF, D])
    win_d = dram("w_in", [D, NIN])
    swo_d = dram("short_w_out", [D, D])
    sso_d = dram("ssm_w_out", [2 * D, D])
    wo_d = dram("w_out", [D, D])
    f2wi_d = dram("ffn2_w_in", [D, 2 * FF])
    f2wo_d = dram("ffn2_w_out", [FF, D])
    out_d = dram("out", [T, D], kind="ExternalOutput")

    es = ExitStack()
    ARENA = 52700
    sb = es.enter_context(nc.sbuf_tensor("sb", [128, ARENA], F32))
    ps = [es.enter_context(nc.psum_tensor("ps%d" % i, [128, 512], F32)) for i in range(8)]
    P = Prog(nc)
    A = Arena(sb, ARENA)

    def kview(ap, k):
        return ap.rearrange("p (k t) -> p k t", k=k)

    xT = kview(A.f32(KD * T), KD)
    vecs = A.f32(NV)
    rep = A.f32(96)
    ident = A.f32(128)
    identb = A.bf16(128)
    onesDb = A.bf16(128)
    ones = A.f32(128)
    Um = A.f32(128)
    SLm = A.f32(128)
    Abc = A.f32(32)
    wdt = kview(A.bf16(KD * 32), KD)
    S32 = kview(A.f32(8 * 256), 8)
    Sbf = kview(A.bf16(8 * 256), 8)
    uhalo = kview(A.f32(8 * 2), 8)
    phalo = kview(A.f32(32 * 3), 32)
    cwh = A.f32(128)
    cbh = A.f32(32)
    nhalf = A.f32(2)[:, 0:1]
    dtb = rep[:, 0:32]
    Dbc = rep[:, 64:96]

    def dve(fn, reads, writes):
        return P.op("dve", fn, reads=reads, writes=writes)

    def act(fn, reads, writes):
        return P.op("act", fn, reads=reads, writes=writes)

    def pool(fn, reads, writes):
        return P.op("pool", fn, reads=reads, writes=writes)

    def pe(fn, reads, writes):
        return P.op("pe", fn, reads=reads, writes=writes)

    def mm(out, lhsT, rhs, start, stop):
        return pe(lambda e: e.matmul(out, lhsT=lhsT, rhs=rhs, start=start, stop=stop),
                  [lhsT, rhs], [out])

    def tr(out, in_, idn):
        return pe(lambda e: e.transpose(out=out, in_=in_, identity=idn), [in_, idn], [out])

    def wload(dst, src):
        return P.dma("pool", dst, src, writes=[dst])

    P.dma("sp", vecs, vecs_d, writes=[vecs])
    P.dma("sp", rep, rep_d, writes=[rep])
    pool(lambda e: e.memset(ones, 1.0), [], [ones])
    pool(lambda e: e.memset(nhalf, -0.5), [], [nhalf])
    pool(lambda e: e.memset(onesDb, 1.0 / D), [], [onesDb])
    pool(lambda e: e.memset(ident, 1.0), [], [ident])
    pool(lambda e: e.affine_select(out=ident, in_=ident, pattern=[[1, 128]], compare_op=ALU.is_equal,
                                   fill=0.0, base=0, channel_multiplier=-1), [ident], [ident])
    pool(lambda e: e.tensor_copy(out=identb, in_=ident), [ident], [identb])
    pool(lambda e: e.memset(Um, 1.0), [], [Um])
    pool(lambda e: e.affine_select(out=Um, in_=Um, pattern=[[1, 128]], compare_op=ALU.is_ge,
                                   fill=0.0, base=0, channel_multiplier=-1), [Um], [Um])
    pool(lambda e: e.memset(SLm, 1.0), [], [SLm])
    pool(lambda e: e.affine_select(out=SLm, in_=SLm, pattern=[[-1, 128]], compare_op=ALU.is_gt,
                                   fill=0.0, base=0, channel_multiplier=1), [SLm], [SLm])
    pool(lambda e: e.memset(S32.rearrange("p g c -> p (g c)"), 0.0), [], [S32])
    pool(lambda e: e.memset(Sbf.rearrange("p g c -> p (g c)"), 0.0), [], [Sbf])
    pool(lambda e: e.memset(uhalo.rearrange("p g c -> p (g c)"), 0.0), [], [uhalo])
    pool(lambda e: e.memset(phalo.rearrange("p g c -> p (g c)"), 0.0), [], [phalo])
    act(lambda e: e.activation(out=Abc, in_=rep[:, 32:64], func=ACTF.Exp), [rep], [Abc])
    dve(lambda e: e.tensor_scalar(out=Abc, in0=Abc, scalar1=-1.0, scalar2=None, op0=ALU.mult), [Abc], [Abc])
    dve(lambda e: e.tensor_scalar(out=cwh, in0=vecs[:, V_CW:V_CW + 128], scalar1=0.5, scalar2=None, op0=ALU.mult), [vecs], [cwh])
    dve(lambda e: e.tensor_scalar(out=cbh, in0=vecs[:, V_CB:V_CB + 32], scalar1=0.5, scalar2=None, op0=ALU.mult), [vecs], [cbh])
    wload(wdt, win_d.rearrange("(k p) c -> p k c", p=128)[:, :, C_DT:C_DT + 32])

    m0 = A.mark()
    xin = [A.f32(D) for _ in range(2)]
    for tt in range(T // 128):
        xi = xin[tt % 2]
        P.dma("sp", xi, x_d[tt * 128:(tt + 1) * 128, :], writes=[xi])
        for b in range(2):
            bank = ps[(tt * 2 + b) % 4]
            for q in range(4):
                k = b * 4 + q
                tr(bank[:, q * 128:(q + 1) * 128], xi[:, k * 128:(k + 1) * 128], ident)
            dst = xT[:, b * 4:(b + 1) * 4, tt * 128:(tt + 1) * 128]
            src = kview(bank[:, :], 4)
            if b == 0:
                act(lambda e, dst=dst, src=src: e.copy(out=dst, in_=src), [src], [dst])
            else:
                dve(lambda e, dst=dst, src=src: e.tensor_copy(out=dst, in_=src), [src], [dst])
    A.release(m0)

    def rmsnorm(t0, ntb, wcol, dst_fn, tmp):
        sq, rt, rstd = tmp
        for tb in range(ntb):
            ts = slice(t0 + tb * 512, t0 + (tb + 1) * 512)
            msp = ps[6 + tb % 2][:, :]
            for k in range(KD):
                s = sq[k % 2]
                act(lambda e, s=s, k=k, ts=ts: e.activation(out=s, in_=xT[:, k, ts], func=ACTF.Square),
                    [xT[:, k, ts]], [s])
                mm(msp, onesDb, s, k == 0, k == KD - 1)
            act(lambda e, msp=msp: e.activation(out=rt, in_=msp, func=ACTF.Ln, bias=EPS, scale=1.0),
                [msp], [rt])
            act(lambda e: e.activation(out=rstd, in_=rt, func=ACTF.Exp, scale=-0.5), [rt], [rstd])
            for k in range(KD):
                dst = dst_fn(k, tb)
                dve(lambda e, dst=dst, k=k, ts=ts: e.scalar_tensor_tensor(
                    out=dst, in0=xT[:, k, ts], scalar=vecs[:, wcol + k:wcol + k + 1], in1=rstd,
                    op0=ALU.mult, op1=ALU.mult), [xT[:, k, ts], vecs, rstd], [dst])

    def norm_tmps():
        return ([A.bf16(512), A.bf16(512)], A.f32(512), A.f32(512))

    def ffn(wcol, wi_d, wo_d_):
        m = A.mark()
        hT = kview(A.bf16(KD * T), KD)
        tmp = norm_tmps()
        rmsnorm(0, 4, wcol, lambda k, tb: hT[:, k, tb * 512:(tb + 1) * 512], tmp)
        pieces = [(0, 4), (4, 4), (8, 4), (12, 4), (16, 3), (19, 3)]
        actb = [A.bf16(4 * T).rearrange("p (j t) -> p j t", j=4) for _ in range(2)]
        wi = [kview(A.bf16(KD * 256), KD) for _ in range(3)]
        wo = [A.bf16(4 * D).rearrange("p (j c) -> p j c", j=4) for _ in range(2)]
        sg = [A.f32(512) for _ in range(2)]
        wi_v = wi_d.rearrange("(k p) c -> p k c", p=128)

        def load_wi(j):
            w = wi[j % 3]
            wload(w[:, :, 0:128], wi_v[:, :, j * 128:(j + 1) * 128])
            wload(w[:, :, 128:256], wi_v[:, :, FF + j * 128:FF + (j + 1) * 128])

        def load_wo(pi):
            c0, n = pieces[pi]
            w = wo[pi % 2]
            wload(w[:, 0:n, :], wo_d_[c0 * 128:(c0 + n) * 128, :].rearrange("(j p) c -> p j c", p=128))

        def win(pi):
            c0, n = pieces[pi]
            ab = actb[pi % 2]
            for jj in range(n):
                j = c0 + jj
                if j + 2 < NF:
                    load_wi(j + 2)
                w = wi[j % 3]
                for tb in range(4):
                    ts = slice(tb * 512, (tb + 1) * 512)
                    G = ps[(j * 4 + tb) % 2][:, :]
                    U = ps[2 + (j * 4 + tb) % 2][:, :]
                    for k in range(KD):
                        mm(G, w[:, k, 0:128], hT[:, k, ts], k == 0, k == KD - 1)
                    for k in range(KD):
                        mm(U, w[:, k, 128:256], hT[:, k, ts], k == 0, k == KD - 1)
                    s = sg[(j * 4 + tb) % 2]
                    act(lambda e, s=s, G=G: e.activation(out=s, in_=G, func=ACTF.Silu), [G], [s])
                    dst = ab[:, jj, ts]
                    dve(lambda e, dst=dst, s=s, U=U: e.tensor_tensor(out=dst, in0=U, in1=s, op=ALU.mult),
                        [U, s], [dst])

        def wout(pi):
            c0, n = pieces[pi]
            ab = actb[pi % 2]
            w = wo[pi % 2]
            last = (pi == len(pieces) - 1)
            order = [(dm, tb) for tb in range(4) for dm in range(KD)] if last else \
                    [(dm, tb) for dm in range(KD) for tb in range(4)]
            for oi, (dm, tb) in enumerate(order):
                if True:
                    ts = slice(tb * 512, (tb + 1) * 512)
                    Y = ps[4 + oi % 2][:, :]
                    for jj in range(n):
                        mm(Y, w[:, jj, dm * 128:(dm + 1) * 128], ab[:, jj, ts], jj == 0, jj == n - 1)
                    dst = xT[:, dm, ts]
                    dve(lambda e, dst=dst, Y=Y: e.scalar_tensor_tensor(out=dst, in0=Y, scalar=0.5, in1=dst,
                                                                      op0=ALU.mult, op1=ALU.add),
                        [Y, dst], [dst])

        load_wi(0)
        load_wi(1)
        load_wo(0)
        win(0)
        for pi in range(1, len(pieces)):
            load_wo(pi) if pi >= 1 else None
            win(pi)
            wout(pi - 1)
        wout(len(pieces) - 1)
        A.release(m)

    if stage >= 1:
        ffn(V_N1, f1wi_d, f1wo_d)

    def mixer():
        m = A.mark()
        TH = 1024
        win_v = win_d.rearrange("(k p) c -> p k c", p=128)
        hT = kview(A.bf16(KD * TH), KD)
        ynT = kview(A.bf16(16 * TH), 16)
        dtw = A.f32(512).rearrange("p (c w h) -> p c w h", c=8, w=2)
        dt_tm = dtw[:, :, 0, :]
        wst = dtw[:, :, 1, :]
        a_tm = kview(A.f32(256), 8)
        eall = kview(A.f32(768), 8)
        m_half = A.mark()
        for hh in range(2):
            A.release(m_half)
            t0 = hh * TH
            ntmp = norm_tmps()
            rmsnorm(t0, 2, V_NM, lambda k, tb: hT[:, k, tb * 512:(tb + 1) * 512], ntmp)
            A.release(m_half)
            dtp = ps[3][:, 0:256]
            for c in range(8):
                for k in range(KD):
                    mm(dtp[:, c * 32:(c + 1) * 32], hT[:, k, c * 128:(c + 1) * 128], wdt[:, k, :], k == 0, k == KD - 1)
            dtp3 = kview(dtp, 8)
            dtb_b = dtb.unsqueeze(1).to_broadcast([128, 8, 32])
            dve(lambda e: e.tensor_tensor(out=dt_tm, in0=dtp3, in1=dtb_b, op=ALU.add), [dtp, rep], [dt_tm])
            act(lambda e: e.activation(out=dt_tm, in_=dt_tm, func=ACTF.Exp), [dt_tm], [dt_tm])
            act(lambda e: e.activation(out=dt_tm, in_=dt_tm, func=ACTF.Ln, bias=1.0, scale=1.0), [dt_tm], [dt_tm])
            A_b = Abc.unsqueeze(1).to_broadcast([128, 8, 32])
            dve(lambda e: e.tensor_tensor(out=a_tm, in0=dt_tm, in1=A_b, op=ALU.mult), [dt_tm, Abc], [a_tm])
            for c in range(8):
                bank = ps[4 + c // 4]
                base = (c % 4) * 96
                mm(bank[:, base:base + 32], Um, a_tm[:, c, :], True, True)
                mm(bank[:, base + 32:base + 64], SLm, a_tm[:, c, :], True, True)
                mm(bank[:, base + 64:base + 96], ones, a_tm[:, c, :], True, True)
            for b2 in range(2):
                src = ps[4 + b2][:, 0:384]
                dst = eall[:, b2 * 4:(b2 + 1) * 4, :].rearrange("p c e -> p (c e)") if False else None
                dstv = eall[:, b2 * 4:(b2 + 1) * 4, :]
                srcv = kview(src, 4)
                act(lambda e, dstv=dstv, srcv=srcv: e.activation(out=dstv, in_=srcv, func=ACTF.Exp), [src], [dstv])
            dve(lambda e: e.tensor_tensor(out=wst, in0=dt_tm, in1=eall[:, :, 32:64], op=ALU.mult),
                [dt_tm, eall], [wst])

            mg = A.mark()
            wproj = kview(A.bf16(KD * 512), KD)
            wz = [kview(A.bf16(KD * 256), KD) for _ in range(2)]
            pre = [A.bf16(4 + TH)[:, 0:3 + TH] for _ in range(2)]
            acc = A.f32(TH)
            thb = A.f32(TH)
            dg2 = [kview(A.bf16(4 * 128), 4) for _ in range(2)]
            fm2 = [kview(A.bf16(4 * TH), 4) for _ in range(2)]
            dmat2 = [kview(A.bf16(4 * 128), 4) for _ in range(2)]
            xb_tm = [A.bf16(384) for _ in range(2)]
            xd2 = [A.bf16(512) for _ in range(2)]
            xdt = [t_[:, 0:256] for t_ in xd2]
            xdtd = [t_[:, 256:512] for t_ in xd2]
            lseg = [kview(A.f32(512), 4) for _ in range(2)]
            Lt = [kview(A.f32(512), 4) for _ in range(2)]
            Mp = [kview(A.bf16(512), 4) for _ in range(2)]
            CBm = [A.f32(128) for _ in range(2)]
            sz = [A.f32(256) for _ in range(2)]
            t1 = A.f32(256)
            yv = A.f32(256)
            gyx2 = [A.f32(258) for _ in range(2)]
            yn = [A.bf16(256) for _ in range(2)]
            junk = A.bf16(258)[:, 0:257]
            for gx_ in gyx2:
                pool(lambda e, gx_=gx_: e.memset(gx_[:, 256:257], 16.0 * (4.0 * EPS) ** 0.5), [], [gx_[:, 256:257]])
            ssb = A.f32(2)
            rsb = A.f32(2)

            def load_wproj(g):
                wload(wproj[:, :, 0:256], win_v[:, :, C_XS + 256 * g:C_XS + 256 * (g + 1)])
                wload(wproj[:, :, 256:384], win_v[:, :, C_BM + 128 * g:C_BM + 128 * (g + 1)])
                wload(wproj[:, :, 384:512], win_v[:, :, C_CM + 128 * g:C_CM + 128 * (g + 1)])

            def load_wz(g):
                wload(wz[g % 2], win_v[:, :, C_Z + 256 * g:C_Z + 256 * (g + 1)])

            def GF(g, pbank):
                fm = fm2[g % 2]
                cbank = tuple(b_ + 4 if b_ < 4 else b_ - 4 for b_ in pbank)
                pieces = []
                for q in range(4):
                    ch = (2 * g + q) if q < 2 else (16 + g if q == 2 else 24 + g)
                    pr = pre[q % 2]
                    dg = dg2[q % 2]

                    def pa0(pr=pr, ch=ch):
                        dve(lambda e: e.tensor_copy(out=pr[:, 0:3], in_=phalo[:, ch, :]), [phalo[:, ch, :]], [pr[:, 0:3]])

                    def pa_tb(tb, pr=pr, q=q):
                        bank = ps[pbank[tb]][:, :]
                        for k in range(KD):
                            mm(bank, wproj[:, k, q * 128:(q + 1) * 128], hT[:, k, tb * 512:(tb + 1) * 512], k == 0, k == KD - 1)
                        dst = pr[:, 3 + tb * 512:3 + (tb + 1) * 512]
                        act(lambda e: e.copy(out=dst, in_=bank), [bank], [dst])

                    def pa2(pr=pr, ch=ch):
                        dve(lambda e: e.tensor_copy(out=phalo[:, ch, :], in_=pr[:, TH:TH + 3]), [pr[:, TH:TH + 3]], [phalo[:, ch, :]])

                    def pdg(dg=dg, ch=ch):
                        dve(lambda e: e.tensor_tensor(
                            out=dg, in0=identb.unsqueeze(1).to_broadcast([128, 4, 128]),
                            in1=cwh[:, 4 * ch:4 * ch + 4].unsqueeze(2).to_broadcast([128, 4, 128]), op=ALU.mult),
                            [identb, cwh], [dg])

                    def pc_tb(tb, pr=pr, dg=dg, ch=ch):
                        bank = ps[cbank[tb]][:, :]
                        for j in range(4):
                            mm(bank, dg[:, j, :], pr[:, j + tb * 512:j + (tb + 1) * 512], j == 0, j == 3)
                        dst = acc[:, tb * 512:(tb + 1) * 512]
                        act(lambda e: e.activation(out=dst, in_=bank, func=ACTF.Identity, bias=cbh[:, ch:ch + 1], scale=1.0),
                            [bank, cbh], [dst])

                    def pb4():
                        act(lambda e: e.activation(out=thb, in_=acc, func=ACTF.Tanh, scale=1.0), [acc], [thb])

                    def pb6(q=q):
                        dve(lambda e: e.scalar_tensor_tensor(out=fm[:, q, :], in0=thb, scalar=1.0, in1=acc,
                                                             op0=ALU.add, op1=ALU.mult),
                            [acc, thb], [fm[:, q, :]])

                    pieces.append([pa0, lambda f=pa_tb: f(0), lambda f=pa_tb: f(1), pa2, pdg])
                    pieces.append([lambda f=pc_tb: f(0), lambda f=pc_tb: f(1), pb4, pb6])
                return pieces

            def S1(it, t):
                g, c = it
                p = t % 2
                fm = fm2[g % 2]
                tc = slice(c * 128, (c + 1) * 128)
                bA, bB, bC = ps[4 * p], ps[4 * p + 1], ps[4 * p + 2]
                trb = bA[:, :].bitcast(BF16)
                xb = xb_tm[p]
                x_tm = xb[:, 0:256]
                x3 = kview(x_tm, 4)
                cbp = bC[:, 256:384]
                zp = bC[:, 0:256]
                cb = CBm[p]
                ls = lseg[p]
                L = Lt[p]
                M = Mp[p]
                szc = sz[p]
                w = wz[g % 2]

                def k0():
                    tr(trb[:, 0:128], fm[:, 0, tc], identb)
                    tr(trb[:, 128:256], fm[:, 1, tc], identb)
                    tr(trb[:, 256:384], fm[:, 2, tc], identb)
                    a_b = a_tm[:, c, 4 * g:4 * g + 4].unsqueeze(2).to_broadcast([128, 4, 128])
                    pool(lambda e: e.tensor_tensor(out=ls, in0=SLm.unsqueeze(1).to_broadcast([128, 4, 128]),
                                                   in1=a_b, op=ALU.mult), [SLm, a_tm], [ls])

                def k1():
                    act(lambda e: e.copy(out=xb, in_=trb[:, 0:384]), [trb], [xb])
                    mm(cbp, fm[:, 2, tc], fm[:, 3, tc], True, True)

                def k2():
                    dve(lambda e: e.tensor_tensor(out=cb, in0=cbp, in1=Um, op=ALU.mult), [cbp, Um], [cb])
                    dw_b = dtw[:, c, :, 4 * g:4 * g + 4].unsqueeze(3).to_broadcast([128, 2, 4, 64])
                    x_b = x3.unsqueeze(1).to_broadcast([128, 2, 4, 64])
                    xo = xd2[p].rearrange("p (w h e) -> p w h e", w=2, h=4)
                    pool(lambda e: e.tensor_tensor(out=xo, in0=x_b, in1=dw_b, op=ALU.mult), [x_tm, dtw[:, c, :, :]], [xd2[p]])
                    for h in range(4):
                        mm(bB[:, h * 128:(h + 1) * 128], ls[:, h, :], Um, True, True)

                def k3():
                    act(lambda e: e.activation(out=L, in_=kview(bB[:, :], 4), func=ACTF.Exp), [bB[:, :]], [L])
                    for k in range(KD):
                        mm(zp, hT[:, k, tc], w[:, k, :], k == 0, k == KD - 1)

                def k4():
                    dve(lambda e: e.tensor_tensor(out=M, in0=L, in1=cb.unsqueeze(1).to_broadcast([128, 4, 128]),
                                                  op=ALU.mult), [L, cb], [M])
                    act(lambda e: e.activation(out=szc, in_=zp, func=ACTF.Tanh, scale=0.5), [zp], [szc])

                def k5():
                    dve(lambda e: e.scalar_tensor_tensor(out=szc, in0=szc, scalar=1.0, in1=zp, op0=ALU.add, op1=ALU.mult),
                        [szc, zp], [szc])

                return [k0, k1, k2, k3, k4, k5]

            def S2(it, t):
                g, c = it
                p = t % 2
                fm = fm2[g % 2]
                dmat = dmat2[g % 2]
                tc = slice(c * 128, (c + 1) * 128)
                bA, bD = ps[4 * p], ps[4 * p + 3]
                xb = xb_tm[p]
                x_tm = xb[:, 0:256]
                B_tm = xb[:, 256:384]
                xd = xdt[p]
                xdd = xdtd[p]
                M = Mp[p]
                szc = sz[p]
                yp = bD[:, 0:256]
                yop = bD[:, 256:512]
                stp = bA[:, 192:448]
                Sg = S32[:, g, :]
                ss = ssb[:, p:p + 1]
                rs = rsb[:, p:p + 1]
                ynn = yn[p]
                gyx = gyx2[p]
                gy = gyx[:, 0:256]

                def k0():
                    for h in range(4):
                        hs = slice(h * 64, (h + 1) * 64)
                        mm(yp[:, hs], M[:, h, :], xd[:, hs], True, False)
                        mm(yp[:, hs], dmat[:, h, :], x_tm[:, hs], False, True)
                    mm(yop, fm[:, 3, tc], Sbf[:, g, :], True, True)
                    mm(stp, B_tm, xdd, True, True)
                    et_b = eall[:, c, 64 + 4 * g:64 + 4 * g + 4].unsqueeze(2).to_broadcast([128, 4, 64])
                    pool(lambda e: e.tensor_tensor(out=kview(Sg, 4), in0=kview(Sg, 4), in1=et_b, op=ALU.mult),
                         [Sg, eall], [Sg])

                def k1():
                    ecs_b = eall[:, c, 4 * g:4 * g + 4].unsqueeze(2).to_broadcast([128, 4, 64])
                    dve(lambda e: e.tensor_tensor(out=kview(t1, 4), in0=kview(yop, 4), in1=ecs_b, op=ALU.mult),
                        [yop, eall], [t1])

                def k2():
                    dve(lambda e: e.tensor_tensor(out=yv, in0=yp, in1=t1, op=ALU.add), [yp, t1], [yv])
                    dve(lambda e: e.tensor_tensor(out=Sg, in0=stp, in1=Sg, op=ALU.add), [stp, Sg], [Sg])

                def k3():
                    dve(lambda e: e.tensor_tensor(out=gy, in0=yv, in1=szc, op=ALU.mult), [yv, szc], [gy])
                    dve(lambda e: e.tensor_copy(out=Sbf[:, g, :], in_=Sg), [Sg], [Sbf[:, g, :]])

                def k4():
                    act(lambda e: e.activation(out=junk, in_=gyx[:, 0:257], func=ACTF.Square, scale=1.0 / 16, accum_out=ss),
                        [gyx[:, 0:257]], [junk, ss])

                def k5():
                    pass

                def k6():
                    pool(lambda e: e.tensor_tensor(out=rs, in0=ss, in1=nhalf, op=ALU.pow), [ss, nhalf], [rs])

                def k7():
                    act(lambda e: e.activation(out=ynn, in_=gy, func=ACTF.Copy, scale=rs), [gy, rs], [ynn])

                return [k0, k1, k2, k3, k4, k5, k6, k7]

            def S3(it, t):
                g, c = it
                p = t % 2
                tc = slice(c * 128, (c + 1) * 128)
                ytp = ps[4 * p + 3][:, 0:128].bitcast(BF16)
                ynn = yn[p]

                def k0():
                    for i in range(2):
                        tr(ytp[:, i * 128:(i + 1) * 128], ynn[:, i * 128:(i + 1) * 128], identb)

                def k1():
                    for i in range(2):
                        dst = ynT[:, 2 * g + i, tc]
                        sc = vecs[:, V_SN + 2 * g + i:V_SN + 2 * g + i + 1]
                        act(lambda e, dst=dst, i=i, sc=sc: e.activation(out=dst, in_=ytp[:, i * 128:(i + 1) * 128],
                                                                       func=ACTF.Copy, scale=sc), [ytp, vecs], [dst])

                return [k0, k1]

            def interleave(lists):
                n = max(len(l) for l in lists)
                for k in range(n):
                    for l in lists:
                        if k < len(l):
                            l[k]()

            def build_dmat(g):
                dm_ = dmat2[g % 2]
                pool(lambda e: e.tensor_tensor(
                    out=dm_, in0=ident.unsqueeze(1).to_broadcast([128, 4, 128]),
                    in1=Dbc[:, 4 * g:4 * g + 4].unsqueeze(2).to_broadcast([128, 4, 128]), op=ALU.mult),
                    [ident, rep], [dm_])

            iters = [(g, c) for g in range(8) for c in range(8)]
            NI = len(iters)
            load_wproj(0)
            load_wz(0)
            for piece in GF(0, (1, 2)):
                for th in piece:
                    th()
            nextgf = None
            for t in range(NI + 2):
                lists = []
                if t < NI:
                    g, c = iters[t]
                    if c == 0:
                        build_dmat(g)
                        if g + 1 < 8:
                            load_wproj(g + 1)
                            load_wz(g + 1)
                            p = t % 2
                            nextgf = GF(g + 1, (4 * (1 - p) + 1, 4 * (1 - p) + 2))
                        else:
                            nextgf = None
                    lists.append(S1(iters[t], t))
                if 0 <= t - 1 < NI:
                    lists.append(S2(iters[t - 1], t - 1))
                if 0 <= t - 2 < NI:
                    lists.append(S3(iters[t - 2], t - 2))
                if t < NI and nextgf is not None:
                    lists.append(nextgf[iters[t][1]])
                interleave(lists)
            A.release(mg)
            ybg = kview(A.bf16(KD * TH), KD)
            mg = A.mark()

            wso = [kview(A.bf16(16 * 128), 16) for _ in range(2)]
            wgb = [kview(A.bf16(KD * 128), KD) for _ in range(2)]
            sgt = [A.f32(512) for _ in range(2)]
            sso_v = sso_d.rearrange("(k p) c -> p k c", p=128)

            def load_b(dm):
                wload(wso[dm % 2], sso_v[:, :, dm * 128:(dm + 1) * 128])
                wload(wgb[dm % 2], win_v[:, :, C_GB + dm * 128:C_GB + (dm + 1) * 128])

            load_b(0)
            for dm in range(KD):
                if dm + 1 < KD:
                    load_b(dm + 1)
                for tb in range(2):
                    ts = slice(tb * 512, (tb + 1) * 512)
                    yb = ps[(dm * 2 + tb) % 2][:, :]
                    gp = ps[2 + (dm * 2 + tb) % 2][:, :]
                    for c16 in range(16):
                        mm(yb, wso[dm % 2][:, c16, :], ynT[:, c16, ts], c16 == 0, c16 == 15)
                    for k in range(KD):
                        mm(gp, wgb[dm % 2][:, k, :], hT[:, k, ts], k == 0, k == KD - 1)
                    s = sgt[(dm * 2 + tb) % 2]
                    act(lambda e, s=s, gp=gp: e.activation(out=s, in_=gp, func=ACTF.Sigmoid), [gp], [s])
                    dst = ybg[:, dm, ts]
                    dve(lambda e, dst=dst, yb=yb, s=s: e.tensor_tensor(out=dst, in0=yb, in1=s, op=ALU.mult), [yb, s], [dst])
            A.release(mg)

            ya_in = ynT[:, 0:8, :]
            ma = A.mark()
            wa3 = [kview(A.bf16(KD * 384), KD) for _ in range(2)]
            ub = [A.f32(2 + TH) for _ in range(2)]
            va = [A.f32(TH) for _ in range(2)]
            cj = [A.f32(512) for _ in range(2)]
            bj = [A.f32(TH) for _ in range(2)]

            def load_a(j):
                w = wa3[j % 2]
                wload(w[:, :, 0:128], win_v[:, :, C_C + j * 128:C_C + (j + 1) * 128])
                wload(w[:, :, 128:256], win_v[:, :, C_XA + j * 128:C_XA + (j + 1) * 128])
                wload(w[:, :, 256:384], win_v[:, :, C_B + j * 128:C_B + (j + 1) * 128])

            load_a(0)
            for j in range(8):
                if j + 1 < 8:
                    load_a(j + 1)
                w = wa3[j % 2]
                u = ub[j % 2]
                v = va[j % 2]
                b_ = bj[j % 2]
                pool(lambda e, u=u, j=j: e.tensor_copy(out=u[:, 0:2], in_=uhalo[:, j, :]), [uhalo[:, j, :]], [u[:, 0:2]])
                for tb in range(2):
                    ts = slice(tb * 512, (tb + 1) * 512)
                    cp = ps[(j * 2 + tb) % 2][:, :]
                    xp = ps[2 + (j * 2 + tb) % 2][:, :]
                    bp = ps[4 + (j * 2 + tb) % 2][:, :]
                    for k in range(KD):
                        mm(cp, w[:, k, 0:128], hT[:, k, ts], k == 0, k == KD - 1)
                    for k in range(KD):
                        mm(xp, w[:, k, 128:256], hT[:, k, ts], k == 0, k == KD - 1)
                    for k in range(KD):
                        mm(bp, w[:, k, 256:384], hT[:, k, ts], k == 0, k == KD - 1)
                    cc = cj[(j * 2 + tb) % 2]
                    act(lambda e, cc=cc, cp=cp: e.copy(out=cc, in_=cp), [cp], [cc])
                    dst = u[:, 2 + tb * 512:2 + (tb + 1) * 512]
                    dve(lambda e, dst=dst, xp=xp, cc=cc: e.tensor_tensor(out=dst, in0=xp, in1=cc, op=ALU.mult), [xp, cc], [dst])
                    bdst = b_[:, ts]
                    act(lambda e, bdst=bdst, bp=bp: e.copy(out=bdst, in_=bp), [bp], [bdst])
                pool(lambda e, u=u, j=j: e.tensor_copy(out=uhalo[:, j, :], in_=u[:, TH:TH + 2]), [u[:, TH:TH + 2]], [uhalo[:, j, :]])
                cw = V_SCW + 3 * j
                dve(lambda e, v=v, u=u, cw=cw: e.tensor_scalar(out=v, in0=u[:, 0:TH], scalar1=vecs[:, cw:cw + 1], scalar2=None,
                                                               op0=ALU.mult), [u, vecs], [v])
                for tpp in range(1, 3):
                    dve(lambda e, v=v, u=u, cw=cw, tpp=tpp: e.scalar_tensor_tensor(
                        out=v, in0=u[:, tpp:tpp + TH], scalar=vecs[:, cw + tpp:cw + tpp + 1], in1=v,
                        op0=ALU.mult, op1=ALU.add), [u, vecs, v], [v])
                dve(lambda e, j=j, v=v, b_=b_: e.tensor_tensor(out=ya_in[:, j, :], in0=v, in1=b_, op=ALU.mult),
                    [v, b_], [ya_in[:, j, :]])
            A.release(ma)
            wsa = [kview(A.bf16(KD * 128), KD) for _ in range(2)]
            wga = [kview(A.bf16(KD * 128), KD) for _ in range(2)]
            swo_v = swo_d.rearrange("(k p) c -> p k c", p=128)
            tg = [A.f32(512) for _ in range(2)]
            sgt = [A.f32(512) for _ in range(2)]

            def load_a2(dm):
                wload(wsa[dm % 2], swo_v[:, :, dm * 128:(dm + 1) * 128])
                wload(wga[dm % 2], win_v[:, :, C_GA + dm * 128:C_GA + (dm + 1) * 128])

            load_a2(0)
            for dm in range(KD):
                if dm + 1 < KD:
                    load_a2(dm + 1)
                for tb in range(2):
                    ts = slice(tb * 512, (tb + 1) * 512)
                    yap = ps[(dm * 2 + tb) % 2][:, :]
                    gp = ps[2 + (dm * 2 + tb) % 2][:, :]
                    for j in range(KD):
                        mm(yap, wsa[dm % 2][:, j, :], ya_in[:, j, ts], j == 0, j == KD - 1)
                    for k in range(KD):
                        mm(gp, wga[dm % 2][:, k, :], hT[:, k, ts], k == 0, k == KD - 1)
                    s = sgt[(dm * 2 + tb) % 2]
                    act(lambda e, s=s, gp=gp: e.activation(out=s, in_=gp, func=ACTF.Sigmoid), [gp], [s])
                    t_ = tg[(dm * 2 + tb) % 2]
                    dve(lambda e, t_=t_, yap=yap, s=s: e.tensor_tensor(out=t_, in0=yap, in1=s, op=ALU.mult), [yap, s], [t_])
                    dst = ybg[:, dm, ts]
                    dve(lambda e, dst=dst, t_=t_: e.tensor_tensor(out=dst, in0=t_, in1=dst, op=ALU.add), [t_, dst], [dst])
            wof = [kview(A.bf16(KD * 128), KD) for _ in range(2)]
            wo_v = wo_d.rearrange("(k p) c -> p k c", p=128)
            wload(wof[0], wo_v[:, :, 0:128])
            for dm in range(KD):
                if dm + 1 < KD:
                    wload(wof[(dm + 1) % 2], wo_v[:, :, (dm + 1) * 128:(dm + 2) * 128])
                for tb in range(2):
                    ts = slice(tb * 512, (tb + 1) * 512)
                    op_ = ps[4 + (dm * 2 + tb) % 2][:, :]
                    for k in range(KD):
                        mm(op_, wof[dm % 2][:, k, :], ybg[:, k, ts], k == 0, k == KD - 1)
                    dst = xT[:, dm, t0 + tb * 512:t0 + (tb + 1) * 512]
                    dve(lambda e, dst=dst, op_=op_: e.tensor_tensor(out=dst, in0=op_, in1=dst, op=ALU.add), [op_, dst], [dst])
        A.release(m)

    if stage >= 2:
        mixer()
    if stage >= 3:
        ffn(V_N2, f2wi_d, f2wo_d)

    m = A.mark()
    hf = [kview(A.f32(KD * 512), KD) for _ in range(1)]
    ntmp = norm_tmps()
    otm = [A.f32(D) for _ in range(2)]
    out_ops = []
    for tb in range(4):
        h = hf[0]
        rmsnorm(tb * 512, 1, V_NF, lambda k, _tb, h=h: h[:, k, :], ntmp)
        for t4 in range(4):
            tt = tb * 4 + t4
            o = otm[tt % 2]
            for b in range(2):
                bank = ps[(tt * 2 + b) % 4]
                for q in range(4):
                    k = b * 4 + q
                    tr(bank[:, q * 128:(q + 1) * 128], h[:, k, t4 * 128:(t4 + 1) * 128], ident)
                dst = o[:, b * 512:(b + 1) * 512]
                if b == 0:
                    act(lambda e, dst=dst, bank=bank: e.copy(out=dst, in_=bank[:, :]), [bank[:, :]], [dst])
                else:
                    dve(lambda e, dst=dst, bank=bank: e.tensor_copy(out=dst, in_=bank[:, :]), [bank[:, :]], [dst])
            out_ops.append(P.dma("sp", out_d[tt * 128:(tt + 1) * 128, :], o, reads=[o]))
    A.release(m)
    P.fence("sp", [o for o in P.ops if o.is_dma])
    if SCHED:
        P.schedule()
    P.emit(es)
    es.close()
    return nc, P


def _pack_inputs(inp):
    f = lambda a: np.ascontiguousarray(np.asarray(a, dtype=np.float32))
    col = lambda v, n: f(v).reshape(n, 128).T
    vecs = np.zeros((128, NV), np.float32)
    vecs[:, V_N1:V_N1 + 8] = col(inp["ffn1_norm"][0], 8)
    vecs[:, V_NM:V_NM + 8] = col(inp["mix_norm"][0], 8)
    vecs[:, V_N2:V_N2 + 8] = col(inp["ffn2_norm"][0], 8)
    vecs[:, V_NF:V_NF + 8] = col(inp["final_norm"], 8)
    scw = f(inp["short_conv_w"][0])
    vecs[:, V_SCW:V_SCW + 24] = scw.reshape(3, 8, 128).transpose(2, 1, 0).reshape(128, 24)
    cw = f(inp["ssm_conv_w"][0])
    vecs[:, V_CW:V_CW + 128] = cw.reshape(4, 32, 128).transpose(2, 1, 0).reshape(128, 128)
    vecs[:, V_CB:V_CB + 32] = col(inp["ssm_conv_b"][0], 32)
    vecs[:, V_SN:V_SN + 16] = col(inp["ssm_norm"][0], 16)
    rep = np.zeros((128, 96), np.float32)
    rep[:, 0:32] = np.broadcast_to(f(inp["ssm_dt_bias"][0]), (128, 32))
    rep[:, 32:64] = np.broadcast_to(f(inp["ssm_A_log"][0]), (128, 32))
    rep[:, 64:96] = np.broadcast_to(f(inp["ssm_D"][0]), (128, 32))
    shared = {"vecs": vecs, "rep": rep}
    for nm in ("ffn1_w_in", "ffn1_w_out", "w_in", "short_w_out", "ssm_w_out", "w_out", "ffn2_w_in", "ffn2_w_out"):
        shared[nm] = f(inp[nm][0])
    return shared


_NC_CACHE = {}


def kernel(**inputs):
    x = np.asarray(inputs["x"], dtype=np.float32)
    nb = x.shape[0]
    shared = _pack_inputs(inputs)
    if "nc" not in _NC_CACHE:
        _NC_CACHE["nc"] = build(3)[0]
    nc = _NC_CACHE["nc"]
    in_maps = []
    for b in range(nb):
        d = dict(shared)
        d["x"] = np.ascontiguousarray(x[b])
        in_maps.append(d)
    res = run_bass_kernel_spmd(nc, in_maps, core_ids=list(range(nb)))
    return np.stack([np.asarray(r["out"], dtype=np.float32) for r in res.results], axis=0)
```
